# Optimizing a Trainium2 kernel written in Bass

```python
import math
import jax, jax.numpy as jnp
from jax import lax
import numpy as np

D_MODEL = 1024
BATCH = 8
SEQ = 4096
DEPTH = 1

D_MIX = D_MODEL
D_CONV = D_MIX // 2
D_ATT = D_MIX - D_CONV
CONV_GROUP_DIM = 64
CONV_GROUPS = D_CONV // CONV_GROUP_DIM
HEAD_DIM = 64
ATT_HEADS = D_ATT // (2 * HEAD_DIM)
D_MIX_IN = 3 * D_CONV + 3 * D_ATT
D_FF = 2816
CONV_WIDTH = 3
Q_BLOCK = 128
NORM_EPS = 1e-6
SUBLN_EPS = 1e-5

kernel_name = "hybrid_shortconv_diffattn_convffn_block"


def rms_norm(x, g, eps=NORM_EPS):
    xf = x.astype(jnp.float32)
    y = xf * lax.rsqrt(jnp.mean(xf * xf, axis=-1, keepdims=True) + eps)
    return (y * g.astype(jnp.float32)).astype(x.dtype)


def dwconv3(x, w, b=None):
    xp = jnp.pad(x, ((0, 0), (1, 1), (0, 0)))
    y = xp[:, :-2] * w[0] + xp[:, 1:-1] * w[1] + xp[:, 2:] * w[2]
    if b is not None:
        y = y + b
    return y


def alibi_slopes(n_heads):
    return 2.0 ** (-8.0 * (jnp.arange(n_heads, dtype=jnp.float32) + 1.0) / n_heads)


def short_conv_mixer(u, conv_w, g_out):
    bsz, seq, _ = u.shape
    b_gate, c_gate, h_in = jnp.split(u, 3, axis=-1)
    y = b_gate * dwconv3(c_gate * h_in, conv_w)
    y = rms_norm(y.reshape(bsz, seq, CONV_GROUPS, CONV_GROUP_DIM),
                 g_out.reshape(CONV_GROUPS, CONV_GROUP_DIM))
    return y.reshape(bsz, seq, D_CONV)


def diff_attention(u, lam, g_subln, lambda_init):
    bsz, seq, _ = u.shape
    q, k, v = jnp.split(u, 3, axis=-1)
    q = q.reshape(bsz, seq, ATT_HEADS, 2, HEAD_DIM)
    k = k.reshape(bsz, seq, ATT_HEADS, 2, HEAD_DIM)
    v = v.reshape(bsz, seq, ATT_HEADS, 2 * HEAD_DIM)
    n_blk = seq // Q_BLOCK
    q_blocks = q.reshape(bsz, n_blk, Q_BLOCK, ATT_HEADS, 2, HEAD_DIM).transpose(1, 0, 2, 3, 4, 5)
    slopes = alibi_slopes(ATT_HEADS)
    k_pos = jnp.arange(seq, dtype=jnp.int32)
    scale = HEAD_DIM ** -0.5

    def one_block(args):
        q_blk, blk_idx = args
        q_pos = blk_idx * Q_BLOCK + jnp.arange(Q_BLOCK, dtype=jnp.int32)
        dist = jnp.abs(q_pos[:, None] - k_pos[None, :]).astype(jnp.float32)
        bias = -slopes[:, None, None] * dist
        s = jnp.einsum('bqhcd,bkhcd->bhcqk', q_blk, k).astype(jnp.float32) * scale
        p = jax.nn.softmax(s + bias[None, :, None], axis=-1)
        p_diff = (p[:, :, 0] - lam * p[:, :, 1]).astype(v.dtype)
        return jnp.einsum('bhqk,bkhe->bqhe', p_diff, v)

    o = lax.map(one_block, (q_blocks, jnp.arange(n_blk, dtype=jnp.int32)))
    o = o.transpose(1, 0, 2, 3, 4).reshape(bsz, seq, ATT_HEADS, 2 * HEAD_DIM)
    o = rms_norm(o, g_subln, eps=SUBLN_EPS) * (1.0 - lambda_init)
    return o.reshape(bsz, seq, D_ATT)


def conv_glu_ffn(h, w_up, conv_w, conv_b, w_down):
    gu = dwconv3(h @ w_up, conv_w, conv_b)
    gate, up = jnp.split(gu, 2, axis=-1)
    return (jax.nn.silu(gate) * up) @ w_down


def setup_inputs(seed: int = 0) -> dict:
    key = jax.random.key(seed)
    ks = jax.random.split(key, 20)
    f32 = jnp.float32

    def nrm(k, shape, scale):
        return jax.random.normal(k, shape, f32) * scale

    def gain(k, shape):
        return 1.0 + 0.05 * jax.random.normal(k, shape, f32)

    L = DEPTH
    return {
        "x": jax.random.normal(ks[0], (BATCH, SEQ, D_MODEL), f32),
        "g_mix_pre": gain(ks[1], (L, D_MODEL)),
        "w_mix_in": nrm(ks[2], (L, D_MODEL, D_MIX_IN), D_MODEL ** -0.5),
        "conv_w": nrm(ks[3], (L, CONV_WIDTH, D_CONV), CONV_WIDTH ** -0.5),
        "g_conv_out": gain(ks[4], (L, D_CONV)),
        "lambda_q1": nrm(ks[5], (L, HEAD_DIM), 0.1),
        "lambda_k1": nrm(ks[6], (L, HEAD_DIM), 0.1),
        "lambda_q2": nrm(ks[7], (L, HEAD_DIM), 0.1),
        "lambda_k2": nrm(ks[8], (L, HEAD_DIM), 0.1),
        "g_subln": gain(ks[9], (L, 2 * HEAD_DIM)),
        "w_mix_out": nrm(ks[10], (L, D_MIX, D_MODEL), D_MIX ** -0.5),
        "g_mix_post": gain(ks[11], (L, D_MODEL)),
        "g_ffn_pre": gain(ks[12], (L, D_MODEL)),
        "w_ffn_up": nrm(ks[13], (L, D_MODEL, 2 * D_FF), D_MODEL ** -0.5),
        "ffn_conv_w": nrm(ks[14], (L, CONV_WIDTH, 2 * D_FF), CONV_WIDTH ** -0.5),
        "ffn_conv_b": nrm(ks[15], (L, 2 * D_FF), 0.02),
        "w_ffn_down": nrm(ks[16], (L, D_FF, D_MODEL), D_FF ** -0.5),
        "g_ffn_post": gain(ks[17], (L, D_MODEL)),
    }


def reference(x, g_mix_pre, w_mix_in, conv_w, g_conv_out, lambda_q1, lambda_k1,
              lambda_q2, lambda_k2, g_subln, w_mix_out, g_mix_post, g_ffn_pre,
              w_ffn_up, ffn_conv_w, ffn_conv_b, w_ffn_down, g_ffn_post):
    for layer in range(DEPTH):
        lambda_init = 0.8 - 0.6 * math.exp(-0.3 * layer)
        lam = (jnp.exp(jnp.sum(lambda_q1[layer].astype(jnp.float32) * lambda_k1[layer].astype(jnp.float32)))
               - jnp.exp(jnp.sum(lambda_q2[layer].astype(jnp.float32) * lambda_k2[layer].astype(jnp.float32)))
               + lambda_init)

        h = rms_norm(x, g_mix_pre[layer])
        u = h @ w_mix_in[layer]
        y_conv = short_conv_mixer(u[..., :3 * D_CONV], conv_w[layer], g_conv_out[layer])
        y_att = diff_attention(u[..., 3 * D_CONV:], lam, g_subln[layer], lambda_init)
        mix = jnp.concatenate([y_conv, y_att], axis=-1) @ w_mix_out[layer]
        x = x + rms_norm(mix, g_mix_post[layer])

        h = rms_norm(x, g_ffn_pre[layer])
        f = conv_glu_ffn(h, w_ffn_up[layer], ffn_conv_w[layer], ffn_conv_b[layer], w_ffn_down[layer])
        x = x + rms_norm(f, g_ffn_post[layer])
    return x
```

```python
import math
from contextlib import ExitStack

import numpy as np
import ml_dtypes
import concourse.bass as bass
import concourse.mybir as mybir
from concourse.bass_utils import run_bass_kernel_spmd

F32 = mybir.dt.float32
BF16 = mybir.dt.bfloat16
ALU = mybir.AluOpType
AF = mybir.ActivationFunctionType

S_LEN = 4096
D = 1024
DFF = 2816
NCH = 8
OWN = 510
NWIN = 9
LAMBDA_INIT = 0.8 - 0.6 * math.exp(-0.3 * 0)

PE, ACT, DVE, POOL, SP = "tensor", "scalar", "vector", "gpsimd", "sync"
ENGS = (PE, ACT, DVE, POOL, SP)
EPOCH = 16000


class Res:
    __slots__ = ("name", "w", "rs", "excl")

    def __init__(self, name, excl=False):
        self.name = name
        self.w = None
        self.rs = []
        self.excl = excl


class Op:
    __slots__ = ("eng", "idx", "fn", "waits", "signal", "dkey", "dcnt", "gcount")

    def __init__(self, eng, idx, fn):
        self.eng = eng
        self.idx = idx
        self.fn = fn
        self.waits = []
        self.signal = False
        self.dkey = None
        self.dcnt = 0
        self.gcount = 0


class Sched:
    def __init__(self, nc):
        self.nc = nc
        self.ops = {e: [] for e in ENGS}
        self.seen = {e: {} for e in ENGS}
        self.dcount = {}
        self.dlast = {}
        self.n_waits = 0

    def _dep(self, op, dep, same_ok):
        if dep is None or dep is op:
            return
        seen = self.seen[op.eng]
        if dep.dkey is not None:
            k = ("d", dep.dkey)
            if seen.get(k, 0) >= dep.dcnt:
                return
            seen[k] = dep.dcnt
            op.waits.append(dep)
            return
        if dep.eng == op.eng and (op.eng == PE or not same_ok):
            return
        if seen.get(dep.eng, -1) >= dep.idx:
            return
        seen[dep.eng] = dep.idx
        dep.signal = True
        op.waits.append(dep)

    def add(self, eng, fn, reads=(), writes=(), dma_key=None):
        lst = self.ops[eng]
        op = Op(eng, len(lst), fn)
        if dma_key is not None:
            c = self.dcount.get(dma_key, 0) + 16
            self.dcount[dma_key] = c
            op.dkey = dma_key
            op.dcnt = c
            self.dlast[dma_key] = op
        for r in reads:
            self._dep(op, r.w, True)
            if r.excl:
                for rd in r.rs:
                    if rd.eng != eng:
                        self._dep(op, rd, True)
        for w in writes:
            self._dep(op, w.w, True)
            for rd in w.rs:
                self._dep(op, rd, True)
        for r in reads:
            r.rs.append(op)
        for w in writes:
            w.w = op
            w.rs = []
        lst.append(op)
        return op

    def barrier(self):
        lasts = []
        for e in ENGS:
            for o in reversed(self.ops[e]):
                if o.dkey is None and o.fn is not None:
                    lasts.append(o)
                    break
        for e in ENGS:
            op = Op(e, len(self.ops[e]), None)
            for d in lasts:
                self._dep(op, d, False)
            for d in self.dlast.values():
                self._dep(op, d, False)
            self.ops[e].append(op)

    def finish(self, eng=SP):
        op = Op(eng, len(self.ops[eng]), None)
        for d in self.dlast.values():
            self._dep(op, d, False)
        self.ops[eng].append(op)

    def emit(self):
        nc = self.nc
        with ExitStack() as st:
            esems = {}
            for e in ENGS:
                g = 0
                for o in self.ops[e]:
                    if o.signal:
                        g += 1
                        o.gcount = g
                nep = (g + EPOCH - 1) // EPOCH
                esems[e] = [st.enter_context(nc.semaphore(f"s_{e}_{i}")) for i in range(nep)]
            dsems = {k: st.enter_context(nc.semaphore(f"d_{i}")) for i, k in enumerate(self.dcount)}

            def run(e, eng):
                for o in self.ops[e]:
                    for d in o.waits:
                        if d.dkey is not None:
                            eng.wait_ge(dsems[d.dkey], d.dcnt)
                        else:
                            ep = (d.gcount - 1) // EPOCH
                            eng.wait_ge(esems[d.eng][ep], d.gcount - ep * EPOCH)
                        self.n_waits += 1
                    if o.fn is None:
                        continue
                    ins = o.fn(eng)
                    if o.dkey is not None:
                        ins.then_inc(dsems[o.dkey], 16)
                    elif o.signal:
                        ep = (o.gcount - 1) // EPOCH
                        ins.then_inc(esems[e][ep], 1)

            with nc.Block() as block:
                @block.tensor
                def _(eng):
                    run(PE, eng)

                @block.scalar
                def _(eng):
                    run(ACT, eng)

                @block.vector
                def _(eng):
                    run(DVE, eng)

                @block.gpsimd
                def _(eng):
                    run(POOL, eng)

                @block.sync
                def _(eng):
                    run(SP, eng)


P_GPRE, P_GPOST, P_GFPRE, P_GFPOST = 0, 8, 16, 24
P_CW, P_GCONV, P_GSUB, P_FW, P_FB = 32, 44, 48, 49, 181
P_LQ1, P_LK1, P_LQ2, P_LK2 = 225, 289, 353, 417
NPAR = 481

CB_ID, CB_OD, CB_OH, CB_O1, CB_B64, CB_CD = 0, 128, 256, 384, 512, 640
NCB = 640 + 4 * 128


def _chunks(v, n):
    return np.ascontiguousarray(v.reshape(n, 128).T)


def pack_params(g_mix_pre, conv_w, g_conv_out, lambda_q1, lambda_k1, lambda_q2, lambda_k2,
                g_subln, g_mix_post, g_ffn_pre, ffn_conv_w, ffn_conv_b, g_ffn_post):
    par = np.zeros((128, NPAR), np.float32)
    par[:, P_GPRE:P_GPRE + 8] = _chunks(g_mix_pre[0], 8)
    par[:, P_GPOST:P_GPOST + 8] = _chunks(g_mix_post[0], 8)
    par[:, P_GFPRE:P_GFPRE + 8] = _chunks(g_ffn_pre[0], 8)
    par[:, P_GFPOST:P_GFPOST + 8] = _chunks(g_ffn_post[0], 8)
    cw = conv_w[0]
    for fc in range(4):
        for tap in range(3):
            par[:, P_CW + fc * 3 + tap] = cw[tap, fc * 128:(fc + 1) * 128]
    par[:, P_GCONV:P_GCONV + 4] = _chunks(g_conv_out[0], 4)
    par[:, P_GSUB] = g_subln[0]
    fw = ffn_conv_w[0]
    for c in range(44):
        for tap in range(3):
            par[:, P_FW + c * 3 + tap] = fw[tap, c * 128:(c + 1) * 128]
    par[:, P_FB:P_FB + 44] = _chunks(ffn_conv_b[0], 44)
    par[:, P_LQ1:P_LQ1 + 64] = lambda_q1[0][None, :]
    par[:, P_LK1:P_LK1 + 64] = lambda_k1[0][None, :]
    par[:, P_LQ2:P_LQ2 + 64] = lambda_q2[0][None, :]
    par[:, P_LK2:P_LK2 + 64] = lambda_k2[0][None, :]
    return par


def const_tables():
    cb = np.zeros((128, NCB), np.float32)
    cb[:, CB_ID:CB_ID + 128] = np.eye(128)
    cb[:, CB_OD:CB_OD + 128] = 1.0 / 1024.0
    cb[:, CB_OH:CB_OH + 128] = 1.0 / 128.0
    cb[:, CB_O1:CB_O1 + 128] = 1.0
    cb[0:64, CB_B64:CB_B64 + 64] = 1.0 / 64.0
    cb[64:128, CB_B64 + 64:CB_B64 + 128] = 1.0 / 64.0
    t = np.arange(S_LEN)
    tb128 = (t // 128) * 128.0
    tr = (t % 128) * 1.0
    kpos = np.zeros((4, 8, S_LEN), np.float32)
    qpa = np.zeros((4, 8, S_LEN), np.float32)
    qpb = np.zeros((4, 8, S_LEN), np.float32)
    p = np.arange(128)[:, None]
    f = np.arange(128)[None, :]
    for h in range(4):
        c = 8.0 * 2.0 ** (-2.0 * (h + 1))
        cb[:, CB_CD + h * 128:CB_CD + (h + 1) * 128] = -2.0 * c * np.maximum(p - f, 0)
        kpos[h, 0] = 1.0
        kpos[h, 1] = 1.0
        kpos[h, 2] = c * tb128
        kpos[h, 3] = c * tr
        kpos[h, 4] = 1.0
        kpos[h, 5] = 1.0
        kpos[h, 6] = -c * tb128
        kpos[h, 7] = -c * tr
        qpa[h, 0] = -c * tb128
        qpa[h, 1] = -c * tr
        qpa[h, 2] = 1.0
        qpa[h, 3] = 1.0
        qpb[h, 4] = c * tb128
        qpb[h, 5] = c * tr
        qpb[h, 6] = 1.0
        qpb[h, 7] = 1.0
    bf = ml_dtypes.bfloat16
    for a in (cb, kpos, qpa, qpb):
        assert np.array_equal(a.astype(bf).astype(np.float32), a)
    return cb.astype(bf), kpos.astype(bf), qpa.astype(bf), qpb.astype(bf)


SB_LO = 16512
SB_HI = 224 * 1024 - 64


class StopBuild(Exception):
    pass


class SBAlloc:
    def __init__(self, nc):
        self.nc = nc
        self.off = SB_LO
        self.n = 0

    def alloc(self, name, shape, dt):
        n = 1
        for s in shape[1:]:
            n *= s
        nbytes = n * (4 if dt == F32 else 2)
        nbytes = (nbytes + 63) // 64 * 64
        assert self.off + nbytes <= SB_HI, (name, self.off, nbytes)
        self.n += 1
        t = self.nc.alloc_sbuf_tensor_at(f"{name}_{self.n}", shape, dt, offset=self.off)
        self.off += nbytes
        return t


def build_program(stop_after=None, n_heads=4):
    nc = bass.Bass("TRN2", target_bir_lowering=False)
    xT = nc.dram_tensor("xT", [D, S_LEN], F32, kind="ExternalInput").ap()
    w_in = nc.dram_tensor("w_in", [D, 3072], F32, kind="ExternalInput").ap()
    w_out = nc.dram_tensor("w_out", [D, D], F32, kind="ExternalInput").ap()
    w_up = nc.dram_tensor("w_up", [D, 2 * DFF], F32, kind="ExternalInput").ap()
    w_dn = nc.dram_tensor("w_dn", [DFF, D], F32, kind="ExternalInput").ap()
    par_d = nc.dram_tensor("par", [128, NPAR], F32, kind="ExternalInput").ap()
    cb_d = nc.dram_tensor("cb", [128, NCB], BF16, kind="ExternalInput").ap()
    kpos_d = nc.dram_tensor("kpos", [4, 8, S_LEN], BF16, kind="ExternalInput").ap()
    qpa_d = nc.dram_tensor("qpa", [4, 8, S_LEN], BF16, kind="ExternalInput").ap()
    qpb_d = nc.dram_tensor("qpb", [4, 8, S_LEN], BF16, kind="ExternalInput").ap()
    yT = nc.dram_tensor("yT", [D, S_LEN], F32, kind="ExternalOutput").ap()
    win_s = nc.dram_tensor("win_s", [128, 8, 3072], BF16, kind="Internal").ap()
    wout_s = nc.dram_tensor("wout_s", [128, 8, D], BF16, kind="Internal").ap()
    wup_s = nc.dram_tensor("wup_s", [128, 8, 2 * DFF], BF16, kind="Internal").ap()
    wdn_s = nc.dram_tensor("wdn_s", [128, 22, D], BF16, kind="Internal").ap()

    xT3 = xT.rearrange("(k p) t -> p k t", p=128)
    yT3 = yT.rearrange("(k p) t -> p k t", p=128)

    S = Sched(nc)
    A = SBAlloc(nc)
    ps = [nc.alloc_psum_tensor(f"ps{i}", [128, 512], F32) for i in range(8)]
    R_ps = [Res(f"ps{i}", excl=True) for i in range(8)]

    par = A.alloc("par", [128, NPAR], F32)
    cbt = A.alloc("cb", [128, NCB], BF16)
    neghalf = A.alloc("neghalf", [128, 512], F32)
    sm = A.alloc("sm", [128, 16], F32)
    lt = A.alloc("lt", [128, 128], F32)
    yconv = A.alloc("yconv", [128, 4, S_LEN + 2], BF16)
    OFF_YA = A.off
    yatt = A.alloc("yatt", [128, 4, S_LEN + 2], BF16)
    OFF_H = A.off
    hT = A.alloc("hT", [128, 8, S_LEN + 2], BF16)
    OFF_T = A.off

    R_par, R_cb, R_nh, R_sm = Res("par"), Res("cb"), Res("nh"), Res("sm")
    R_h = [Res(f"h{c}") for c in range(NCH)]
    R_hpad = Res("hpad")
    R_yc = [[Res(f"yc{fc}_{c}") for c in range(NCH)] for fc in range(4)]
    R_ya = [[Res(f"ya{h}_{c}") for c in range(NCH)] for h in range(4)]
    R_ypad = Res("ypad")
    R_scr = {k: Res(k) for k in ("win_s", "wout_s", "wup_s", "wdn_s")}

    ident = cbt[:, CB_ID:CB_ID + 128]
    ones_d = cbt[:, CB_OD:CB_OD + 128]
    ones_h = cbt[:, CB_OH:CB_OH + 128]
    ones1 = cbt[:, CB_O1:CB_O1 + 128]
    blk64 = cbt[:, CB_B64:CB_B64 + 128]

    def pcol(c):
        return par[:, c:c + 1]

    S.add(SP, lambda e: e.dma_start(out=par[:, :], in_=par_d), writes=[R_par], dma_key="par")
    S.add(SP, lambda e: e.dma_start(out=cbt[:, :], in_=cb_d), writes=[R_cb], dma_key="cb")
    S.add(DVE, lambda e: e.memset(neghalf[:, :], -0.5), writes=[R_nh])
    R_lt = Res("lt")
    S.add(DVE, lambda e: e.tensor_tensor(lt[:, 0:64], par[:, P_LQ1:P_LQ1 + 64], par[:, P_LK1:P_LK1 + 64], op=ALU.mult),
          reads=[R_par], writes=[R_lt])
    S.add(DVE, lambda e: e.tensor_tensor(lt[:, 64:128], par[:, P_LQ2:P_LQ2 + 64], par[:, P_LK2:P_LK2 + 64], op=ALU.mult),
          reads=[R_par], writes=[R_lt])
    S.add(DVE, lambda e: e.reduce_sum(sm[:, 2:3], lt[:, 0:64], axis=mybir.AxisListType.X), reads=[R_lt], writes=[R_sm])
    S.add(DVE, lambda e: e.reduce_sum(sm[:, 3:4], lt[:, 64:128], axis=mybir.AxisListType.X), reads=[R_lt], writes=[R_sm])
    S.add(ACT, lambda e: e.activation(sm[:, 4:6], sm[:, 2:4], AF.Exp), reads=[R_sm], writes=[R_sm])
    S.add(DVE, lambda e: e.scalar_tensor_tensor(sm[:, 0:1], sm[:, 5:6], -LAMBDA_INIT, sm[:, 4:5], op0=ALU.add, op1=ALU.subtract),
          reads=[R_sm], writes=[R_sm])
    S.add(DVE, lambda e: e.tensor_scalar(sm[:, 1:2], par[:, P_GSUB:P_GSUB + 1], 1.0 - LAMBDA_INIT, None, op0=ALU.mult),
          reads=[R_par, R_sm], writes=[R_sm])
    for buf, nk, rr in ((hT, 8, R_hpad), (yconv, 4, R_ypad)):
        S.add(POOL, lambda e, buf=buf: e.memset(buf[:, :, 0:1], 0.0), writes=[rr])
        S.add(POOL, lambda e, buf=buf: e.memset(buf[:, :, S_LEN + 1:S_LEN + 2], 0.0), writes=[rr])

    S.add(DVE, lambda e: e.memset(sm[:, 6:7], 1e-6), writes=[R_sm])
    S.add(DVE, lambda e: e.memset(sm[:, 7:8], 1e-5), writes=[R_sm])

    def rstd_from(bank, ncols, eps, msb, R_msb):
        ecol = sm[:, 6:7] if eps == 1e-6 else sm[:, 7:8]
        S.add(ACT, lambda e: e.activation(msb[:, 0:ncols], ps[bank][:, 0:ncols], AF.Ln, bias=ecol, scale=1.0),
              reads=[R_ps[bank], R_sm], writes=[R_msb])
        S.add(ACT, lambda e: e.activation(msb[:, 0:ncols], msb[:, 0:ncols], AF.Exp, scale=-0.5),
              reads=[R_msb], writes=[R_msb])

    A.off = OFF_T
    xt = [A.alloc(f"xt{i}", [128, 8, 512], F32) for i in range(2)]
    sqb = [A.alloc(f"sqb{i}", [128, 8, 512], BF16) for i in range(2)]
    msb = [A.alloc(f"msb{i}", [128, 512], F32) for i in range(2)]
    R_xt = [Res("xt0"), Res("xt1")]
    R_sqb = [Res("sqb0"), Res("sqb1")]
    R_msb = [Res("msb0"), Res("msb1")]
    def p1a(c):
        b = c % 2
        S.add(SP, lambda e, c=c, b=b: e.dma_start(out=xt[b][:, :, :], in_=xT3[:, :, c * 512:(c + 1) * 512]),
              writes=[R_xt[b]], dma_key=f"xt{b}")
        for k in range(8):
            S.add(ACT, lambda e, b=b, k=k: e.activation(sqb[b][:, k, :], xt[b][:, k, :], AF.Square),
                  reads=[R_xt[b]], writes=[R_sqb[b]])
        for k in range(8):
            S.add(PE, lambda e, b=b, k=k: e.matmul(ps[7][:, :], ones_d, sqb[b][:, k, :], start=(k == 0), stop=(k == 7)),
                  reads=[R_sqb[b], R_cb], writes=[R_ps[7]])
        rstd_from(7, 512, 1e-6, msb[b], R_msb[b])
        outs = []
        for k in range(8):
            outs.append(lambda b=b, k=k, c=c: S.add(DVE, lambda e: e.scalar_tensor_tensor(
                hT[:, k, 1 + c * 512:1 + (c + 1) * 512], xt[b][:, k, :], pcol(P_GPRE + k), msb[b][:, :],
                op0=ALU.mult, op1=ALU.mult),
                reads=[R_xt[b], R_msb[b], R_par], writes=[R_h[c]]))
        return outs

    def win(w):
        c0 = OWN * w
        n = min(512, S_LEN + 2 - c0)
        return c0, n

    def chunks_of(c0, n):
        t0 = max(c0 - 1, 0)
        t1 = min(c0 + n - 2, S_LEN - 1)
        return list(range(t0 // 512, t1 // 512 + 1))

    off_p1a_end = A.off
    A.off = OFF_YA
    wc = A.alloc("wc", [128, 8, 1536], BF16)
    R_wc = Res("wc")
    for k in range(8):
        S.add(POOL, lambda e, k=k: e.dma_start(out=wc[:, k, :], in_=w_in[k * 128:(k + 1) * 128, 0:1536]),
              writes=[R_wc], dma_key="wc")
    for k in range(8):
        S.add(POOL, lambda e, k=k: e.dma_start(out=win_s[:, k, 1536:3072], in_=w_in[k * 128:(k + 1) * 128, 1536:3072]),
              writes=[R_scr["win_s"]], dma_key="win_s")
    NB1 = 2
    hsb = [A.alloc(f"hsb{i}", [128, 512], F32) for i in range(NB1)]
    chb = [A.alloc(f"chb{i}", [128, 512], F32) for i in range(NB1)]
    assert A.off <= OFF_H
    A.off = off_p1a_end
    zb = [A.alloc(f"zb{i}", [128, 512], F32) for i in range(NB1)]
    yb = [A.alloc(f"yb{i}", [128, 512], F32) for i in range(NB1)]
    ysq = [A.alloc(f"ysq{i}", [128, 512], BF16) for i in range(NB1)]
    rb = [A.alloc(f"rb{i}", [128, 512], F32) for i in range(NB1)]
    R_hsb = [Res(f"hsb{i}") for i in range(NB1)]
    R_chb = [Res(f"chb{i}") for i in range(NB1)]
    R_zb = [Res(f"zb{i}") for i in range(NB1)]
    R_yb = [Res(f"yb{i}") for i in range(NB1)]
    R_ysq = [Res(f"ysq{i}") for i in range(NB1)]
    R_rb = [Res(f"rb{i}") for i in range(NB1)]

    its = [(w, fc) for w in range(NWIN) for fc in range(4)]

    def p1b_mm(i):
        w, fc = its[i]
        c0, n = win(w)
        hres = [R_h[c] for c in chunks_of(c0, n)] + [R_hpad]
        for g in range(3):
            bank = (i % 2) * 3 + g
            for k in range(8):
                S.add(PE, lambda e, bank=bank, g=g, k=k, fc=fc, c0=c0, n=n: e.matmul(
                    ps[bank][:, 0:n], wc[:, k, g * 512 + fc * 128:g * 512 + (fc + 1) * 128], hT[:, k, c0:c0 + n],
                    start=(k == 0), stop=(k == 7)),
                    reads=[R_wc] + hres, writes=[R_ps[bank]])

    def p1b_chain(i):
        w, fc = its[i]
        c0, n = win(w)
        b = i % NB1
        bB, bC, bH = (i % 2) * 3, (i % 2) * 3 + 1, (i % 2) * 3 + 2
        m = n - 2
        S.add(ACT, lambda e: e.activation(hsb[b][:, 0:n], ps[bH][:, 0:n], AF.Copy), reads=[R_ps[bH]], writes=[R_hsb[b]])
        S.add(DVE, lambda e: e.tensor_tensor(chb[b][:, 0:n], ps[bC][:, 0:n], hsb[b][:, 0:n], op=ALU.mult),
              reads=[R_ps[bC], R_hsb[b]], writes=[R_chb[b]])
        S.add(DVE, lambda e: e.tensor_scalar(zb[b][:, 0:m], chb[b][:, 1:n - 1], pcol(P_CW + fc * 3 + 1), None, op0=ALU.mult),
              reads=[R_chb[b], R_par], writes=[R_zb[b]])
        S.add(DVE, lambda e: e.scalar_tensor_tensor(zb[b][:, 0:m], chb[b][:, 0:m], pcol(P_CW + fc * 3 + 0), zb[b][:, 0:m],
                                                     op0=ALU.mult, op1=ALU.add),
              reads=[R_chb[b], R_zb[b], R_par], writes=[R_zb[b]])
        S.add(DVE, lambda e: e.scalar_tensor_tensor(zb[b][:, 0:m], chb[b][:, 2:n], pcol(P_CW + fc * 3 + 2), zb[b][:, 0:m],
                                                    op0=ALU.mult, op1=ALU.add),
              reads=[R_chb[b], R_zb[b], R_par], writes=[R_zb[b]])
        S.add(DVE, lambda e: e.tensor_tensor(yb[b][:, 0:m], ps[bB][:, 1:n - 1], zb[b][:, 0:m], op=ALU.mult),
              reads=[R_ps[bB], R_zb[b]], writes=[R_yb[b]])
        S.add(ACT, lambda e: e.activation(ysq[b][:, 0:m], yb[b][:, 0:m], AF.Square), reads=[R_yb[b]], writes=[R_ysq[b]])
        sb_ = 6
        S.add(PE, lambda e: e.matmul(ps[sb_][:, 0:m], blk64, ysq[b][:, 0:m], start=True, stop=True),
              reads=[R_ysq[b], R_cb], writes=[R_ps[sb_]])

    def p1b_final(i):
        w, fc = its[i]
        c0, n = win(w)
        b = i % NB1
        m = n - 2
        rstd_from(6, m, 1e-6, rb[b], R_rb[b])
        yres = [R_yc[fc][c] for c in chunks_of(c0 + 1, m)]
        S.add(DVE, lambda e: e.scalar_tensor_tensor(yconv[:, fc, c0 + 1:c0 + 1 + m], yb[b][:, 0:m], pcol(P_GCONV + fc),
                                                    rb[b][:, 0:m], op0=ALU.mult, op1=ALU.mult),
              reads=[R_yb[b], R_rb[b], R_par], writes=yres)

    def p1b_iter(i):
        if i == 0:
            p1b_mm(0)
        if i + 1 < len(its):
            p1b_mm(i + 1)
        if i > 0:
            p1b_final(i - 1)
        p1b_chain(i)

    for f in p1a(0):
        f()
    for c in range(1, NCH):
        hw = p1a(c)
        for k_, i in enumerate(range(4 * (c - 1), 4 * c)):
            if k_ == 3:
                for f in hw:
                    f()
                hw = []
            p1b_iter(i)
            for f in hw[:3]:
                f()
            hw = hw[3:]
    for i in range(4 * (NCH - 1), len(its)):
        p1b_iter(i)
    p1b_final(len(its) - 1)
    if stop_after == "p1b":
        S.finish(); S.emit(); return nc
    S.barrier()
    S.add(POOL, lambda e: e.memset(yatt[:, :, 0:1], 0.0), writes=[R_ypad])
    S.add(POOL, lambda e: e.memset(yatt[:, :, S_LEN + 1:S_LEN + 2], 0.0), writes=[R_ypad])

    A.off = OFF_T
    kT = [A.alloc(f"kT{m}", [128, S_LEN], BF16) for m in range(2)]
    vt = A.alloc("vt", [128, S_LEN], BF16)
    wq = [A.alloc(f"wq{i}", [128, 8, 384], BF16) for i in range(2)]
    qA = [[A.alloc(f"qA{m}{b}", [128, 512], BF16) for b in range(2)] for m in range(2)]
    qB = [[A.alloc(f"qB{m}{b}", [128, 512], BF16) for b in range(2)] for m in range(2)]
    NPT = 6
    pT = [A.alloc(f"pT{i}", [128, 512], BF16) for i in range(NPT)]
    ocp = [A.alloc(f"ocp{m}", [128, 512], F32) for m in range(2)]
    zcp = [A.alloc(f"zcp{m}", [128, 512], F32) for m in range(2)]
    osq = A.alloc("osq", [128, 512], BF16)
    rr = A.alloc("rr", [128, 512], F32)
    R_kd = [[Res(f"kd{m}_{c}") for c in range(NCH)] for m in range(2)]
    R_kp = [Res("kp0"), Res("kp1")]
    R_vt = [Res(f"vt{g}") for g in range(NCH)]
    R_wq = [Res("wq0"), Res("wq1")]
    R_qd = [[Res(f"qd{m}{b}") for b in range(2)] for m in range(2)]
    R_qp = [[Res(f"qp{m}{b}") for b in range(2)] for m in range(2)]
    R_pT = [Res(f"pT{i}") for i in range(NPT)]
    R_ocp = [Res("ocp0"), Res("ocp1")]
    R_zcp = [Res("zcp0"), Res("zcp1")]
    R_osq, R_rr = Res("osq"), Res("rr")
    OB, ZB = (4, 5), (6, 7)
    sctr = [0]
    sbank_of = {}

    def next_sbank():
        b_ = sctr[0] % 4
        sctr[0] += 1
        return b_

    def load_wq(h):
        b = h % 2
        for i, col in enumerate((1536, 2048, 2560)):
            S.add(SP, lambda e, b=b, i=i, col=col, h=h: e.dma_start(
                out=wq[b][:, :, i * 128:(i + 1) * 128], in_=win_s[:, :, col + h * 128:col + (h + 1) * 128]),
                reads=[R_scr["win_s"]], writes=[R_wq[b]], dma_key=f"wq{b}")

    def kv_gen(h):
        b = h % 2
        for m in range(2):
            S.add(SP, lambda e, m=m, h=h: e.dma_start(out=kT[m][64:72, :], in_=kpos_d[h]),
                  writes=[R_kp[m]], dma_key=f"kp{m}")
        for c in range(NCH):
            bank = next_sbank()
            for k in range(8):
                S.add(PE, lambda e, bank=bank, k=k, c=c: e.matmul(
                    ps[bank][:, :], wq[b][:, k, 128:256], hT[:, k, 1 + c * 512:1 + (c + 1) * 512],
                    start=(k == 0), stop=(k == 7)),
                    reads=[R_wq[b], R_h[c]], writes=[R_ps[bank]])
            S.add(ACT, lambda e, bank=bank, c=c: e.activation(kT[0][0:64, c * 512:(c + 1) * 512], ps[bank][0:64, :], AF.Copy),
                  reads=[R_ps[bank]], writes=[R_kd[0][c]])
            S.add(DVE, lambda e, bank=bank, c=c: e.tensor_copy(kT[1][0:64, c * 512:(c + 1) * 512], ps[bank][64:128, :]),
                  reads=[R_ps[bank]], writes=[R_kd[1][c]])
        for g in range(NCH):
            bank = next_sbank()
            for j in range(4):
                tb = 4 * g + j
                for k in range(8):
                    S.add(PE, lambda e, bank=bank, k=k, j=j, tb=tb: e.matmul(
                        ps[bank][:, j * 128:(j + 1) * 128], hT[:, k, 1 + tb * 128:1 + (tb + 1) * 128], wq[b][:, k, 256:384],
                        start=(k == 0), stop=(k == 7)),
                        reads=[R_wq[b], R_h[g]], writes=[R_ps[bank]])
            if g % 2 == 0:
                S.add(ACT, lambda e, bank=bank, g=g: e.activation(vt[:, g * 512:(g + 1) * 512], ps[bank][:, :], AF.Copy),
                      reads=[R_ps[bank]], writes=[R_vt[g]])
            else:
                S.add(DVE, lambda e, bank=bank, g=g: e.tensor_copy(vt[:, g * 512:(g + 1) * 512], ps[bank][:, :]),
                      reads=[R_ps[bank]], writes=[R_vt[g]])

    def q_pos(h, qc):
        b = qc % 2
        for m in range(2):
            S.add(SP, lambda e, m=m, b=b, h=h, qc=qc: e.dma_start(out=qA[m][b][64:72, :], in_=qpa_d[h, :, qc * 512:(qc + 1) * 512]),
                  writes=[R_qp[m][b]], dma_key=f"qp{m}{b}")
            S.add(SP, lambda e, m=m, b=b, h=h, qc=qc: e.dma_start(out=qB[m][b][64:72, :], in_=qpb_d[h, :, qc * 512:(qc + 1) * 512]),
                  writes=[R_qp[m][b]], dma_key=f"qp{m}{b}")

    def q_gen(h, qc):
        b = qc % 2
        wb = h % 2
        GB = next_sbank()
        for k in range(8):
            S.add(PE, lambda e, k=k: e.matmul(ps[GB][:, :], wq[wb][:, k, 0:128], hT[:, k, 1 + qc * 512:1 + (qc + 1) * 512],
                                              start=(k == 0), stop=(k == 7)),
                  reads=[R_wq[wb], R_h[qc]], writes=[R_ps[GB]])
        for m in range(2):
            S.add(DVE, lambda e, m=m: e.tensor_copy(qA[m][b][0:64, :], ps[GB][m * 64:(m + 1) * 64, :]),
                  reads=[R_ps[GB]], writes=[R_qd[m][b]])
            S.add(DVE, lambda e, m=m: e.tensor_copy(qB[m][b][0:64, :], ps[GB][m * 64:(m + 1) * 64, :]),
                  reads=[R_ps[GB]], writes=[R_qd[m][b]])

    def qk(h, qc, i):
        kb, m = i // 2, i % 2
        b = qc % 2
        bank = next_sbank()
        sbank_of[(h, qc, i)] = bank
        j = kb - 4 * qc
        lhs = kT[m][0:72, kb * 128:(kb + 1) * 128]
        rd = [R_kd[m][kb // 4], R_kp[m], R_qd[m][b], R_qp[m][b]]

        def mmv(lo, hi, qt, start=True, stop=True):
            S.add(PE, lambda e: e.matmul(ps[bank][:, lo:hi], lhs, qt[0:72, lo:hi], start=start, stop=stop),
                  reads=rd, writes=[R_ps[bank]])
        if j < 0:
            mmv(0, 512, qA[m][b])
        elif j > 3:
            mmv(0, 512, qB[m][b])
        else:
            if j > 0:
                mmv(0, 128 * j, qB[m][b])
            mmv(128 * j, 128 * (j + 1), qA[m][b], True, False)
            S.add(PE, lambda e: e.matmul(ps[bank][:, 128 * j:128 * (j + 1)], ident, cbt[:, CB_CD + h * 128:CB_CD + (h + 1) * 128],
                                         start=False, stop=True),
                  reads=[R_cb], writes=[R_ps[bank]])
            if j < 3:
                mmv(128 * (j + 1), 512, qA[m][b])

    def expo(h, qc, i):
        bank = sbank_of[(h, qc, i)]
        pb = i % NPT
        S.add(ACT, lambda e: e.activation(pT[pb][:, :], ps[bank][:, :], AF.Exp, scale=0.125),
              reads=[R_ps[bank]], writes=[R_pT[pb]])

    def pv(i):
        kb, m = i // 2, i % 2
        pb = i % NPT
        S.add(PE, lambda e: e.matmul(ps[OB[m]][:, :], vt[:, kb * 128:(kb + 1) * 128], pT[pb][:, :], start=(kb == 0), stop=(kb == 31)),
              reads=[R_vt[kb // 4], R_pT[pb]], writes=[R_ps[OB[m]]])
        S.add(PE, lambda e: e.matmul(ps[ZB[m]][:, :], ones1, pT[pb][:, :], start=(kb == 0), stop=(kb == 31)),
              reads=[R_cb, R_pT[pb]], writes=[R_ps[ZB[m]]])

    def fin_copy(m):
        S.add(DVE, lambda e: e.tensor_copy(ocp[m][:, :], ps[OB[m]][:, :]), reads=[R_ps[OB[m]]], writes=[R_ocp[m]])
        S.add(DVE, lambda e: e.tensor_copy(zcp[m][:, :], ps[ZB[m]][:, :]), reads=[R_ps[ZB[m]]], writes=[R_zcp[m]])

    def fin_rest(h, qc):
        for m in range(2):
            S.add(DVE, lambda e, m=m: e.reciprocal(zcp[m][:, :], zcp[m][:, :]), reads=[R_zcp[m]], writes=[R_zcp[m]])
            S.add(DVE, lambda e, m=m: e.tensor_tensor(ocp[m][:, :], ocp[m][:, :], zcp[m][:, :], op=ALU.mult),
                  reads=[R_ocp[m], R_zcp[m]], writes=[R_ocp[m]])
        S.add(DVE, lambda e: e.scalar_tensor_tensor(ocp[0][:, :], ocp[1][:, :], sm[:, 0:1], ocp[0][:, :], op0=ALU.mult, op1=ALU.add),
              reads=[R_ocp[0], R_ocp[1], R_sm], writes=[R_ocp[0]])
        S.add(DVE, lambda e: e.tensor_tensor(osq[:, :], ocp[0][:, :], ocp[0][:, :], op=ALU.mult), reads=[R_ocp[0]], writes=[R_osq])

    def fin_rest2(h, qc):
        GB = next_sbank()
        S.add(PE, lambda e: e.matmul(ps[GB][:, :], ones_h, osq[:, :], start=True, stop=True), reads=[R_osq, R_cb], writes=[R_ps[GB]])
        rstd_from(GB, 512, 1e-5, rr, R_rr)
        S.add(DVE, lambda e: e.scalar_tensor_tensor(yatt[:, h, 1 + qc * 512:1 + (qc + 1) * 512], ocp[0][:, :], sm[:, 1:2], rr[:, :],
                                                    op0=ALU.mult, op1=ALU.mult),
              reads=[R_ocp[0], R_rr, R_sm], writes=[R_ya[h][qc]])

    for k in range(8):
        S.add(POOL, lambda e, k=k: e.dma_start(out=wout_s[:, k, :], in_=w_out[k * 128:(k + 1) * 128, :]),
              writes=[R_scr["wout_s"]], dma_key="wout_s")
    for k in range(8):
        S.add(POOL, lambda e, k=k: e.dma_start(out=wup_s[:, k, :], in_=w_up[k * 128:(k + 1) * 128, :]),
              writes=[R_scr["wup_s"]], dma_key="wup_s")
    for j in range(22):
        S.add(POOL, lambda e, j=j: e.dma_start(out=wdn_s[:, j, :], in_=w_dn[j * 128:(j + 1) * 128, :]),
              writes=[R_scr["wdn_s"]], dma_key="wdn_s")
    LA = 3
    load_wq(0)
    pending = None
    for h in range(n_heads):
        if h + 1 < n_heads:
            load_wq(h + 1)
        if pending is not None:
            fin_rest(*pending)
        kv_gen(h)
        if pending is not None:
            fin_rest2(*pending)
            pending = None
        q_pos(h, 0)
        q_gen(h, 0)
        for i in range(LA):
            qk(h, 0, i)
        for qc in range(NCH):
            if qc + 1 < NCH:
                q_pos(h, qc + 1)
            for i in range(64):
                expo(h, qc, i)
                pv(i)
                if i >= 62:
                    fin_copy(i - 62)
                if i + LA < 64:
                    qk(h, qc, i + LA)
                elif qc + 1 < NCH:
                    qk(h, qc + 1, i + LA - 64)
                if i == 2 and pending is not None:
                    fin_rest(*pending)
                if i == 26 and pending is not None:
                    fin_rest2(*pending)
                    pending = None
                if i == 36 and qc + 1 < NCH:
                    q_gen(h, qc + 1)
            pending = (h, qc)
    fin_rest(*pending)
    fin_rest2(*pending)
    if stop_after == "p2":
        S.finish(); S.emit(); return nc
    S.barrier()

    A.off = OFF_H
    xw = [A.alloc(f"xw{i}", [128, 8, 512], F32) for i in range(2)]
    mb = [A.alloc(f"mb{i}", [128, 8, 512], F32) for i in range(2)]
    h2 = A.alloc("h2", [128, 8, 512], BF16)
    at = A.alloc("at", [128, 22, 512], BF16)
    NWT = 5
    wt = [A.alloc(f"wt{i}", [128, 8, 128], BF16) for i in range(NWT)]
    wd = [A.alloc(f"wd{i}", [128, 22, 128], BF16) for i in range(2)]
    g0 = [A.alloc(f"g0{i}", [128, 512], F32) for i in range(2)]
    u0 = [A.alloc(f"u0{i}", [128, 512], F32) for i in range(2)]
    NSQ = 2
    sqt = [A.alloc(f"sqt{i}", [128, 512], BF16) for i in range(NSQ)]
    rs3 = [A.alloc(f"rs3{i}", [128, 512], F32) for i in range(2)]
    rs3 = [rs3[0], rs3[0], rs3[1]]
    sqd = [A.alloc(f"sqd{i}", [128, 512], BF16) for i in range(4)]
    R_sqd = [Res(f"sqd{i}") for i in range(4)]
    R_xw = [[Res(f"xw{i}_{m}") for m in range(8)] for i in range(2)]
    R_mb = [[Res(f"mb{i}_{m}") for m in range(8)] for i in range(2)]
    R_h2 = [Res(f"h2{m}") for m in range(8)]
    R_at = [Res(f"at{j}") for j in range(22)]
    R_wt = [Res(f"wt{i}") for i in range(NWT)]
    R_wd = [Res("wd0"), Res("wd1")]
    R_g0 = [Res("g00"), Res("g01")]
    R_u0 = [Res("u00"), Res("u01")]
    R_sqt = [Res(f"sqt{i}") for i in range(NSQ)]
    sqx = [A.alloc(f"sqx{i}", [128, 512], BF16) for i in range(3)]
    R_sqx = [Res(f"sqx{i}") for i in range(3)]
    sqxc = [0]
    R_rs3 = [Res(f"rs3{i}") for i in range(2)]
    R_rs3 = [R_rs3[0], R_rs3[0], R_rs3[1]]
    sqc = [0]

    def next_sq():
        i_ = sqc[0] % NSQ
        sqc[0] += 1
        return i_

    wseq = []
    for w in range(NWIN):
        for m in range(8):
            wseq.append(("wout_s", wout_s[:, :, m * 128:(m + 1) * 128]))
        for j in range(22):
            wseq.append(("wup_s", wup_s[:, :, j * 128:(j + 1) * 128]))
            wseq.append(("wup_s", wup_s[:, :, DFF + j * 128:DFF + (j + 1) * 128]))
    wissued = [0]
    WLA = 4

    def wtile(idx):
        while wissued[0] < len(wseq) and wissued[0] <= idx + WLA:
            k_ = wissued[0]
            b_ = k_ % NWT
            key, src = wseq[k_]
            S.add(SP, lambda e, b_=b_, src=src: e.dma_start(out=wt[b_][:, :, :], in_=src),
                  reads=[R_scr[key]], writes=[R_wt[b_]], dma_key=f"wt{b_}")
            wissued[0] += 1
        return wt[idx % NWT], R_wt[idx % NWT]

    def widx_m(w, m):
        return w * 52 + m

    def widx_u(w, j, which):
        return w * 52 + 8 + 2 * j + which

    dissued = [0]

    def dtile(idx):
        while dissued[0] < NWIN * 8 and dissued[0] <= idx + 1:
            k_ = dissued[0]
            b_ = k_ % 2
            m = k_ % 8
            S.add(SP, lambda e, b_=b_, m=m: e.dma_start(out=wd[b_][:, :, :], in_=wdn_s[:, :, m * 128:(m + 1) * 128]),
                  reads=[R_scr["wdn_s"]], writes=[R_wd[b_]], dma_key=f"wd{b_}")
            dissued[0] += 1
        return wd[idx % 2], R_wd[idx % 2]

    def geom(w):
        c0, n = win(w)
        return c0, n, n - 2

    def p3_LX(w):
        c0, n, m_ = geom(w)
        xb = xw[w % 2]
        tok_lo = max(c0 - 1, 0)
        tok_hi = min(c0 - 1 + n, S_LEN)
        col_lo = tok_lo - (c0 - 1)
        ncol = tok_hi - tok_lo
        if col_lo > 0:
            S.add(POOL, lambda e: e.memset(xb[:, :, 0:col_lo], 0.0), writes=R_xw[w % 2])
        if col_lo + ncol < n:
            S.add(POOL, lambda e: e.memset(xb[:, :, col_lo + ncol:n], 0.0), writes=R_xw[w % 2])
        S.add(SP, lambda e: e.dma_start(out=xb[:, :, col_lo:col_lo + ncol], in_=xT3[:, :, tok_lo:tok_hi]),
              writes=R_xw[w % 2], dma_key=f"xw{w % 2}")

    def stat_mm(bank, si, ncols, first, last, pool=None):
        tl, rl = (sqt, R_sqt) if pool is None else pool
        S.add(PE, lambda e: e.matmul(ps[bank][:, 0:ncols], ones_d, tl[si][:, 0:ncols], start=first, stop=last),
              reads=[rl[si], R_cb], writes=[R_ps[bank]])

    qc_ = []
    q3_ = []

    def flush(q, keep=0):
        while len(q) > keep:
            q.pop(0)()

    def p3_M(w):
        c0, n, m_ = geom(w)
        mbw, R_mbw = mb[w % 2], R_mb[w % 2]
        ycr = [[R_yc[k][c] for c in chunks_of(c0, n)] + [R_ypad] for k in range(4)]
        yar = [[R_ya[k][c] for c in chunks_of(c0, n)] + [R_ypad] for k in range(4)]
        prev = None
        for m in range(8):
            bank = m % 4
            wtl, rw = wtile(widx_m(w, m))
            for k in range(8):
                src = yconv if k < 4 else yatt
                rres = ycr[k] if k < 4 else yar[k - 4]
                S.add(PE, lambda e, bank=bank, k=k, src=src, wtl=wtl: e.matmul(
                    ps[bank][:, 0:n], wtl[:, k, :], src[:, k % 4, c0:c0 + n], start=(k == 0), stop=(k == 7)),
                    reads=[rw] + rres, writes=[R_ps[bank]])
            flush(qc_, 1)
            S.add(DVE, lambda e, bank=bank, m=m: e.tensor_copy(mbw[:, m, 0:n], ps[bank][:, 0:n]),
                  reads=[R_ps[bank]], writes=[R_mbw[m]])
            si = next_sq()
            S.add(ACT, lambda e, m=m, si=si: e.activation(sqt[si][:, 0:n], mbw[:, m, 0:n], AF.Square),
                  reads=[R_mbw[m]], writes=[R_sqt[si]])
            qc_.append(lambda si=si, m=m: stat_mm(6, si, n, m == 0, m == 7))

    def chain_slices(w):
        c0, n, m_ = geom(w)
        xb, R_xb = xw[w % 2], R_xw[w % 2]
        mbw, R_mbw = mb[w % 2], R_mb[w % 2]
        sl = []
        sis = {}

        def x1_ops(ms, with_rstd):
            def f():
                flush(qc_)
                if with_rstd:
                    rstd_from(6, n, 1e-6, rs3[0], R_rs3[0])
                for m in ms:
                    S.add(POOL, lambda e, m=m: e.tensor_tensor(mbw[:, m, 0:n], mbw[:, m, 0:n], rs3[0][:, 0:n], op=ALU.mult),
                          reads=[R_mbw[m], R_rs3[0]], writes=[R_mbw[m]])
                    S.add(DVE, lambda e, m=m: e.scalar_tensor_tensor(xb[:, m, 0:n], mbw[:, m, 0:n], pcol(P_GPOST + m), xb[:, m, 0:n],
                                                                     op0=ALU.mult, op1=ALU.add),
                          reads=[R_mbw[m], R_xb[m], R_par], writes=[R_xb[m]])
                    si = sqxc[0] % 3
                    sqxc[0] += 1
                    sis[m] = si
                    S.add(ACT, lambda e, m=m, si=si: e.activation(sqx[si][:, 0:n], xb[:, m, 0:n], AF.Square),
                          reads=[R_xb[m]], writes=[R_sqx[si]])
            return f

        def x1_pe(ms):
            def f():
                for m in ms:
                    qc_.append(lambda m=m: stat_mm(7, sis[m], n, m == 0, m == 7, pool=(sqx, R_sqx)))
            return f

        def h2_ops(ms, with_rstd):
            def f():
                if with_rstd:
                    flush(qc_)
                    rstd_from(7, n, 1e-6, rs3[1], R_rs3[1])
                for m in ms:
                    S.add(DVE, lambda e, m=m: e.scalar_tensor_tensor(h2[:, m, 0:n], xb[:, m, 0:n], pcol(P_GFPRE + m), rs3[1][:, 0:n],
                                                                     op0=ALU.mult, op1=ALU.mult),
                          reads=[R_xb[m], R_rs3[1], R_par], writes=[R_h2[m]])
            return f
        sl.append((x1_ops([0, 1], True), x1_pe([0, 1])))
        sl.append((x1_ops([2, 3, 4], False), x1_pe([2, 3, 4])))
        sl.append((x1_ops([5, 6, 7], False), x1_pe([5, 6, 7])))
        sl.append((h2_ops([0, 1, 2, 3], True), None))
        sl.append((h2_ops([4, 5, 6, 7], False), None))
        return sl

    def p3_U(w):
        c0, n, m_ = geom(w)

        def up_mm(j):
            for which, bank in ((0, (j % 2) * 2), (1, (j % 2) * 2 + 1)):
                wtl, rw = wtile(widx_u(w, j, which))
                for k in range(8):
                    S.add(PE, lambda e, bank=bank, wtl=wtl, k=k: e.matmul(ps[bank][:, 0:n], wtl[:, k, :], h2[:, k, 0:n],
                                                                          start=(k == 0), stop=(k == 7)),
                          reads=[rw, R_h2[k]], writes=[R_ps[bank]])

        def up_ew(j):
            b2 = j % 2
            bg, bu = (j % 2) * 2, (j % 2) * 2 + 1
            for (bank, dst, rd, cidx) in ((bg, g0[b2], R_g0[b2], j), (bu, u0[b2], R_u0[b2], 22 + j)):
                S.add(ACT, lambda e, bank=bank, dst=dst, cidx=cidx: e.activation(
                    dst[:, 0:m_], ps[bank][:, 1:n - 1], AF.Identity, bias=pcol(P_FB + cidx), scale=pcol(P_FW + cidx * 3 + 1)),
                    reads=[R_ps[bank], R_par], writes=[rd])
                S.add(DVE, lambda e, bank=bank, dst=dst, cidx=cidx: e.scalar_tensor_tensor(
                    dst[:, 0:m_], ps[bank][:, 0:m_], pcol(P_FW + cidx * 3 + 0), dst[:, 0:m_], op0=ALU.mult, op1=ALU.add),
                    reads=[R_ps[bank], rd, R_par], writes=[rd])
                S.add(DVE, lambda e, bank=bank, dst=dst, cidx=cidx: e.scalar_tensor_tensor(
                    dst[:, 0:m_], ps[bank][:, 2:n], pcol(P_FW + cidx * 3 + 2), dst[:, 0:m_], op0=ALU.mult, op1=ALU.add),
                    reads=[R_ps[bank], rd, R_par], writes=[rd])
            S.add(ACT, lambda e: e.activation(g0[b2][:, 0:m_], g0[b2][:, 0:m_], AF.Silu), reads=[R_g0[b2]], writes=[R_g0[b2]])
            S.add(POOL, lambda e: e.tensor_tensor(at[:, j, 0:m_], g0[b2][:, 0:m_], u0[b2][:, 0:m_], op=ALU.mult),
                  reads=[R_g0[b2], R_u0[b2]], writes=[R_at[j]])

        def head():
            up_mm(0)
            up_mm(1)

        def rest(hooks=()):
            hooks = list(hooks)
            for j in range(22):
                hk = hooks.pop(0) if hooks else (None, None)
                if hk[0] is not None:
                    hk[0]()
                up_ew(j)
                if j + 2 < 22:
                    up_mm(j + 2)
                if hk[1] is not None:
                    hk[1]()
            assert not hooks
        return head, rest

    def p3_D(w, slices):
        c0, n, m_ = geom(w)
        mbw, R_mbw = mb[w % 2], R_mb[w % 2]
        have_next = w + 1 < NWIN
        rstd1_done = not have_next
        for m in range(8):
            if m == 2 and have_next:
                p3_M(w + 1)
            bank = 4 + (m % 2)
            wdl, rwd = dtile(w * 8 + m)
            for j in range(22):
                S.add(PE, lambda e, bank=bank, wdl=wdl, j=j: e.matmul(ps[bank][:, 0:m_], wdl[:, j, :], at[:, j, 0:m_],
                                                                      start=(j == 0), stop=(j == 21)),
                      reads=[rwd, R_at[j]], writes=[R_ps[bank]])
            flush(qc_, 1)
            if rstd1_done:
                flush(q3_, 1)
            S.add(DVE, lambda e, bank=bank, m=m: e.tensor_copy(mbw[:, m, 1:1 + m_], ps[bank][:, 0:m_]),
                  reads=[R_ps[bank]], writes=[R_mbw[m]])
            si = m % 4
            S.add(ACT, lambda e, m=m, si=si: e.activation(sqd[si][:, 0:m_], mbw[:, m, 1:1 + m_], AF.Square),
                  reads=[R_mbw[m]], writes=[R_sqd[si]])
            if m >= 2 and slices:
                nf, pf = slices.pop(0)
                nf()
                if pf is not None:
                    pf()
                rstd1_done = True
            q3_.append(lambda si=si, m=m: stat_mm(6, si, m_, m == 0, m == 7, pool=(sqd, R_sqd)))
        assert not slices

    def p3_E(w):
        c0, n, m_ = geom(w)
        xb, R_xb = xw[w % 2], R_xw[w % 2]
        mbw, R_mbw = mb[w % 2], R_mb[w % 2]

        def head():
            flush(qc_)
            flush(q3_)
            rstd_from(6, m_, 1e-6, rs3[2], R_rs3[2])

        def part(m):
            def pre():
                S.add(POOL, lambda e: e.tensor_tensor(mbw[:, m, 1:1 + m_], mbw[:, m, 1:1 + m_], rs3[2][:, 0:m_], op=ALU.mult),
                      reads=[R_mbw[m], R_rs3[2]], writes=[R_mbw[m]])

            def post():
                S.add(DVE, lambda e: e.scalar_tensor_tensor(mbw[:, m, 1:1 + m_], mbw[:, m, 1:1 + m_], pcol(P_GFPOST + m),
                                                            xb[:, m, 1:1 + m_], op0=ALU.mult, op1=ALU.add),
                      reads=[R_mbw[m], R_xb[m], R_par], writes=[R_mbw[m]])
            return (pre, post)

        def tail():
            t0 = OWN * w
            S.add(SP, lambda e: e.dma_start(out=yT3[:, :, t0:t0 + m_], in_=mbw[:, :, 1:1 + m_]),
                  reads=R_mbw, dma_key=f"out{w % 2}")
        return [(head, None)] + [part(m) for m in range(8)] + [(None, tail)]

    p3_LX(0)
    p3_M(0)
    for nf, pf in chain_slices(0):
        nf()
        if pf is not None:
            pf()
    flush(qc_)
    p3_LX(1)
    uh, ur = p3_U(0)
    uh()
    ur()
    for w in range(NWIN):
        sl = chain_slices(w + 1) if w + 1 < NWIN else []
        p3_D(w, sl)
        steps = p3_E(w)
        if w + 1 < NWIN:
            uh, ur = p3_U(w + 1)
            uh()
            if w + 2 < NWIN:
                steps.append((None, lambda w=w: p3_LX(w + 2)))
            ur(steps)
        else:
            for pre_, post_ in steps:
                if pre_ is not None:
                    pre_()
                if post_ is not None:
                    post_()

    S.finish()
    S.emit()
    return nc


_CACHE = {}


def kernel(x, g_mix_pre, w_mix_in, conv_w, g_conv_out, lambda_q1, lambda_k1, lambda_q2, lambda_k2,
           g_subln, w_mix_out, g_mix_post, g_ffn_pre, w_ffn_up, ffn_conv_w, ffn_conv_b, w_ffn_down,
           g_ffn_post):
    f32 = np.float32
    x = np.asarray(x, f32)
    par = pack_params(*[np.asarray(a, f32) for a in (
        g_mix_pre, conv_w, g_conv_out, lambda_q1, lambda_k1, lambda_q2, lambda_k2, g_subln,
        g_mix_post, g_ffn_pre, ffn_conv_w, ffn_conv_b, g_ffn_post)])
    cb, kpos, qpa, qpb = const_tables()
    w_in = np.ascontiguousarray(np.asarray(w_mix_in, f32)[0])
    w_out = np.ascontiguousarray(np.asarray(w_mix_out, f32)[0])
    w_up = np.ascontiguousarray(np.asarray(w_ffn_up, f32)[0])
    w_dn = np.ascontiguousarray(np.asarray(w_ffn_down, f32)[0])
    if "nc" not in _CACHE:
        _CACHE["nc"] = build_program()
    nc = _CACHE["nc"]
    in_maps = []
    for b in range(8):
        in_maps.append({
            "xT": np.ascontiguousarray(x[b].T),
            "w_in": w_in, "w_out": w_out, "w_up": w_up, "w_dn": w_dn,
            "par": par, "cb": cb, "kpos": kpos, "qpa": qpa, "qpb": qpb,
        })
    res = run_bass_kernel_spmd(nc, in_maps, core_ids=list(range(8)))
    out = np.stack([np.asarray(r["yT"], f32).T for r in res.results], axis=0)
    return np.ascontiguousarray(out)
```

```python
import math
from contextlib import ExitStack

import numpy as np
import ml_dtypes
import concourse.bass as bass
import concourse.mybir as mybir
from concourse.bass_utils import run_bass_kernel_spmd

F32 = mybir.dt.float32
BF16 = mybir.dt.bfloat16
ALU = mybir.AluOpType
AF = mybir.ActivationFunctionType

S_LEN = 4096
D = 1024
DFF = 2816
NCH = 8
OWN = 510
NWIN = 9
LAMBDA_INIT = 0.8 - 0.6 * math.exp(-0.3 * 0)

PE, ACT, DVE, POOL, SP = "tensor", "scalar", "vector", "gpsimd", "sync"
ENGS = (PE, ACT, DVE, POOL, SP)
EPOCH = 16000


class Res:
    __slots__ = ("name", "w", "rs", "excl")

    def __init__(self, name, excl=False):
        self.name = name
        self.w = None
        self.rs = []
        self.excl = excl


class Op:
    __slots__ = ("eng", "idx", "fn", "waits", "signal", "dkey", "dcnt", "gcount")

    def __init__(self, eng, idx, fn):
        self.eng = eng
        self.idx = idx
        self.fn = fn
        self.waits = []
        self.signal = False
        self.dkey = None
        self.dcnt = 0
        self.gcount = 0


class Sched:
    def __init__(self, nc):
        self.nc = nc
        self.ops = {e: [] for e in ENGS}
        self.seen = {e: {} for e in ENGS}
        self.dcount = {}
        self.dlast = {}
        self.n_waits = 0

    def _dep(self, op, dep, same_ok):
        if dep is None or dep is op:
            return
        seen = self.seen[op.eng]
        if dep.dkey is not None:
            k = ("d", dep.dkey)
            if seen.get(k, 0) >= dep.dcnt:
                return
            seen[k] = dep.dcnt
            op.waits.append(dep)
            return
        if dep.eng == op.eng and (op.eng == PE or not same_ok):
            return
        if seen.get(dep.eng, -1) >= dep.idx:
            return
        seen[dep.eng] = dep.idx
        dep.signal = True
        op.waits.append(dep)

    def add(self, eng, fn, reads=(), writes=(), dma_key=None):
        lst = self.ops[eng]
        op = Op(eng, len(lst), fn)
        if dma_key is not None:
            c = self.dcount.get(dma_key, 0) + 16
            self.dcount[dma_key] = c
            op.dkey = dma_key
            op.dcnt = c
            self.dlast[dma_key] = op
        for r in reads:
            self._dep(op, r.w, True)
            if r.excl:
                for rd in r.rs:
                    if rd.eng != eng:
                        self._dep(op, rd, True)
        for w in writes:
            self._dep(op, w.w, True)
            for rd in w.rs:
                self._dep(op, rd, True)
        for r in reads:
            r.rs.append(op)
        for w in writes:
            w.w = op
            w.rs = []
        lst.append(op)
        return op

    def barrier(self):
        lasts = []
        for e in ENGS:
            for o in reversed(self.ops[e]):
                if o.dkey is None and o.fn is not None:
                    lasts.append(o)
                    break
        for e in ENGS:
            op = Op(e, len(self.ops[e]), None)
            for d in lasts:
                self._dep(op, d, False)
            for d in self.dlast.values():
                self._dep(op, d, False)
            self.ops[e].append(op)

    def finish(self, eng=SP):
        op = Op(eng, len(self.ops[eng]), None)
        for d in self.dlast.values():
            self._dep(op, d, False)
        self.ops[eng].append(op)

    def emit(self):
        nc = self.nc
        with ExitStack() as st:
            esems = {}
            for e in ENGS:
                g = 0
                for o in self.ops[e]:
                    if o.signal:
                        g += 1
                        o.gcount = g
                nep = (g + EPOCH - 1) // EPOCH
                esems[e] = [st.enter_context(nc.semaphore(f"s_{e}_{i}")) for i in range(nep)]
            dsems = {k: st.enter_context(nc.semaphore(f"d_{i}")) for i, k in enumerate(self.dcount)}

            def run(e, eng):
                for o in self.ops[e]:
                    for d in o.waits:
                        if d.dkey is not None:
                            eng.wait_ge(dsems[d.dkey], d.dcnt)
                        else:
                            ep = (d.gcount - 1) // EPOCH
                            eng.wait_ge(esems[d.eng][ep], d.gcount - ep * EPOCH)
                        self.n_waits += 1
                    if o.fn is None:
                        continue
                    ins = o.fn(eng)
                    if o.dkey is not None:
                        ins.then_inc(dsems[o.dkey], 16)
                    elif o.signal:
                        ep = (o.gcount - 1) // EPOCH
                        ins.then_inc(esems[e][ep], 1)

            with nc.Block() as block:
                @block.tensor
                def _(eng):
                    run(PE, eng)

                @block.scalar
                def _(eng):
                    run(ACT, eng)

                @block.vector
                def _(eng):
                    run(DVE, eng)

                @block.gpsimd
                def _(eng):
                    run(POOL, eng)

                @block.sync
                def _(eng):
                    run(SP, eng)


P_GPRE, P_GPOST, P_GFPRE, P_GFPOST = 0, 8, 16, 24
P_CW, P_GCONV, P_GSUB, P_FW, P_FB = 32, 44, 48, 49, 181
P_LQ1, P_LK1, P_LQ2, P_LK2 = 225, 289, 353, 417
NPAR = 481

CB_ID, CB_OD, CB_OH, CB_O1, CB_B64, CB_CD = 0, 128, 256, 384, 512, 640
NCB = 640 + 4 * 128


def _chunks(v, n):
    return np.ascontiguousarray(v.reshape(n, 128).T)


def pack_params(g_mix_pre, conv_w, g_conv_out, lambda_q1, lambda_k1, lambda_q2, lambda_k2,
                g_subln, g_mix_post, g_ffn_pre, ffn_conv_w, ffn_conv_b, g_ffn_post):
    par = np.zeros((128, NPAR), np.float32)
    par[:, P_GPRE:P_GPRE + 8] = _chunks(g_mix_pre[0], 8)
    par[:, P_GPOST:P_GPOST + 8] = _chunks(g_mix_post[0], 8)
    par[:, P_GFPRE:P_GFPRE + 8] = _chunks(g_ffn_pre[0], 8)
    par[:, P_GFPOST:P_GFPOST + 8] = _chunks(g_ffn_post[0], 8)
    cw = conv_w[0]
    for fc in range(4):
        for tap in range(3):
            par[:, P_CW + fc * 3 + tap] = cw[tap, fc * 128:(fc + 1) * 128]
    par[:, P_GCONV:P_GCONV + 4] = _chunks(g_conv_out[0], 4)
    par[:, P_GSUB] = g_subln[0]
    fw = ffn_conv_w[0]
    for c in range(44):
        for tap in range(3):
            par[:, P_FW + c * 3 + tap] = fw[tap, c * 128:(c + 1) * 128]
    par[:, P_FB:P_FB + 44] = _chunks(ffn_conv_b[0], 44)
    par[:, P_LQ1:P_LQ1 + 64] = lambda_q1[0][None, :]
    par[:, P_LK1:P_LK1 + 64] = lambda_k1[0][None, :]
    par[:, P_LQ2:P_LQ2 + 64] = lambda_q2[0][None, :]
    par[:, P_LK2:P_LK2 + 64] = lambda_k2[0][None, :]
    return par


def const_tables():
    cb = np.zeros((128, NCB), np.float32)
    cb[:, CB_ID:CB_ID + 128] = np.eye(128)
    cb[:, CB_OD:CB_OD + 128] = 1.0 / 1024.0
    cb[:, CB_OH:CB_OH + 128] = 1.0 / 128.0
    cb[:, CB_O1:CB_O1 + 128] = 1.0
    cb[0:64, CB_B64:CB_B64 + 64] = 1.0 / 64.0
    cb[64:128, CB_B64 + 64:CB_B64 + 128] = 1.0 / 64.0
    t = np.arange(S_LEN)
    tb128 = (t // 128) * 128.0
    tr = (t % 128) * 1.0
    kpos = np.zeros((4, 8, S_LEN), np.float32)
    qpa = np.zeros((4, 8, S_LEN), np.float32)
    qpb = np.zeros((4, 8, S_LEN), np.float32)
    p = np.arange(128)[:, None]
    f = np.arange(128)[None, :]
    for h in range(4):
        c = 8.0 * 2.0 ** (-2.0 * (h + 1))
        cb[:, CB_CD + h * 128:CB_CD + (h + 1) * 128] = -2.0 * c * np.maximum(p - f, 0)
        kpos[h, 0] = 1.0
        kpos[h, 1] = 1.0
        kpos[h, 2] = c * tb128
        kpos[h, 3] = c * tr
        kpos[h, 4] = 1.0
        kpos[h, 5] = 1.0
        kpos[h, 6] = -c * tb128
        kpos[h, 7] = -c * tr
        qpa[h, 0] = -c * tb128
        qpa[h, 1] = -c * tr
        qpa[h, 2] = 1.0
        qpa[h, 3] = 1.0
        qpb[h, 4] = c * tb128
        qpb[h, 5] = c * tr
        qpb[h, 6] = 1.0
        qpb[h, 7] = 1.0
    bf = ml_dtypes.bfloat16
    for a in (cb, kpos, qpa, qpb):
        assert np.array_equal(a.astype(bf).astype(np.float32), a)
    return cb.astype(bf), kpos.astype(bf), qpa.astype(bf), qpb.astype(bf)


SB_LO = 16512
SB_HI = 224 * 1024 - 64


class StopBuild(Exception):
    pass


class SBAlloc:
    def __init__(self, nc):
        self.nc = nc
        self.off = SB_LO
        self.n = 0

    def alloc(self, name, shape, dt):
        n = 1
        for s in shape[1:]:
            n *= s
        nbytes = n * (4 if dt == F32 else 2)
        nbytes = (nbytes + 63) // 64 * 64
        assert self.off + nbytes <= SB_HI, (name, self.off, nbytes)
        self.n += 1
        t = self.nc.alloc_sbuf_tensor_at(f"{name}_{self.n}", shape, dt, offset=self.off)
        self.off += nbytes
        return t


def build_program(stop_after=None, n_heads=4):
    nc = bass.Bass("TRN2", target_bir_lowering=False)
    xT = nc.dram_tensor("xT", [D, S_LEN], F32, kind="ExternalInput").ap()
    w_in = nc.dram_tensor("w_in", [D, 3072], F32, kind="ExternalInput").ap()
    w_out = nc.dram_tensor("w_out", [D, D], F32, kind="ExternalInput").ap()
    w_up = nc.dram_tensor("w_up", [D, 2 * DFF], F32, kind="ExternalInput").ap()
    w_dn = nc.dram_tensor("w_dn", [DFF, D], F32, kind="ExternalInput").ap()
    par_d = nc.dram_tensor("par", [128, NPAR], F32, kind="ExternalInput").ap()
    cb_d = nc.dram_tensor("cb", [128, NCB], BF16, kind="ExternalInput").ap()
    kpos_d = nc.dram_tensor("kpos", [4, 8, S_LEN], BF16, kind="ExternalInput").ap()
    qpa_d = nc.dram_tensor("qpa", [4, 8, S_LEN], BF16, kind="ExternalInput").ap()
    qpb_d = nc.dram_tensor("qpb", [4, 8, S_LEN], BF16, kind="ExternalInput").ap()
    yT = nc.dram_tensor("yT", [D, S_LEN], F32, kind="ExternalOutput").ap()
    win_s = nc.dram_tensor("win_s", [128, 8, 3072], BF16, kind="Internal").ap()
    wout_s = nc.dram_tensor("wout_s", [128, 8, D], BF16, kind="Internal").ap()
    wup_s = nc.dram_tensor("wup_s", [128, 8, 2 * DFF], BF16, kind="Internal").ap()
    wdn_s = nc.dram_tensor("wdn_s", [128, 22, D], BF16, kind="Internal").ap()

    xT3 = xT.rearrange("(k p) t -> p k t", p=128)
    yT3 = yT.rearrange("(k p) t -> p k t", p=128)

    S = Sched(nc)
    A = SBAlloc(nc)
    ps = [nc.alloc_psum_tensor(f"ps{i}", [128, 512], F32) for i in range(8)]
    R_ps = [Res(f"ps{i}", excl=True) for i in range(8)]

    par = A.alloc("par", [128, NPAR], F32)
    cbt = A.alloc("cb", [128, NCB], BF16)
    neghalf = A.alloc("neghalf", [128, 512], F32)
    sm = A.alloc("sm", [128, 16], F32)
    lt = A.alloc("lt", [128, 128], F32)
    yconv = A.alloc("yconv", [128, 4, S_LEN + 2], BF16)
    OFF_YA = A.off
    yatt = A.alloc("yatt", [128, 4, S_LEN + 2], BF16)
    OFF_H = A.off
    hT = A.alloc("hT", [128, 8, S_LEN + 2], BF16)
    OFF_T = A.off

    R_par, R_cb, R_nh, R_sm = Res("par"), Res("cb"), Res("nh"), Res("sm")
    R_h = [Res(f"h{c}") for c in range(NCH)]
    R_hpad = Res("hpad")
    R_yc = [[Res(f"yc{fc}_{c}") for c in range(NCH)] for fc in range(4)]
    R_ya = [[Res(f"ya{h}_{c}") for c in range(NCH)] for h in range(4)]
    R_ypad = Res("ypad")
    R_scr = {k: Res(k) for k in ("win_s", "wout_s", "wup_s", "wdn_s")}

    ident = cbt[:, CB_ID:CB_ID + 128]
    ones_d = cbt[:, CB_OD:CB_OD + 128]
    ones_h = cbt[:, CB_OH:CB_OH + 128]
    ones1 = cbt[:, CB_O1:CB_O1 + 128]
    blk64 = cbt[:, CB_B64:CB_B64 + 128]

    def pcol(c):
        return par[:, c:c + 1]

    S.add(SP, lambda e: e.dma_start(out=par[:, :], in_=par_d), writes=[R_par], dma_key="par")
    S.add(SP, lambda e: e.dma_start(out=cbt[:, :], in_=cb_d), writes=[R_cb], dma_key="cb")
    S.add(DVE, lambda e: e.memset(neghalf[:, :], -0.5), writes=[R_nh])
    R_lt = Res("lt")
    S.add(DVE, lambda e: e.tensor_tensor(lt[:, 0:64], par[:, P_LQ1:P_LQ1 + 64], par[:, P_LK1:P_LK1 + 64], op=ALU.mult),
          reads=[R_par], writes=[R_lt])
    S.add(DVE, lambda e: e.tensor_tensor(lt[:, 64:128], par[:, P_LQ2:P_LQ2 + 64], par[:, P_LK2:P_LK2 + 64], op=ALU.mult),
          reads=[R_par], writes=[R_lt])
    S.add(DVE, lambda e: e.reduce_sum(sm[:, 2:3], lt[:, 0:64], axis=mybir.AxisListType.X), reads=[R_lt], writes=[R_sm])
    S.add(DVE, lambda e: e.reduce_sum(sm[:, 3:4], lt[:, 64:128], axis=mybir.AxisListType.X), reads=[R_lt], writes=[R_sm])
    S.add(ACT, lambda e: e.activation(sm[:, 4:6], sm[:, 2:4], AF.Exp), reads=[R_sm], writes=[R_sm])
    S.add(DVE, lambda e: e.scalar_tensor_tensor(sm[:, 0:1], sm[:, 5:6], -LAMBDA_INIT, sm[:, 4:5], op0=ALU.add, op1=ALU.subtract),
          reads=[R_sm], writes=[R_sm])
    S.add(DVE, lambda e: e.tensor_scalar(sm[:, 1:2], par[:, P_GSUB:P_GSUB + 1], 1.0 - LAMBDA_INIT, None, op0=ALU.mult),
          reads=[R_par, R_sm], writes=[R_sm])
    for buf, nk, rr in ((hT, 8, R_hpad), (yconv, 4, R_ypad)):
        S.add(POOL, lambda e, buf=buf: e.memset(buf[:, :, 0:1], 0.0), writes=[rr])
        S.add(POOL, lambda e, buf=buf: e.memset(buf[:, :, S_LEN + 1:S_LEN + 2], 0.0), writes=[rr])

    S.add(DVE, lambda e: e.memset(sm[:, 6:7], 1e-6), writes=[R_sm])
    S.add(DVE, lambda e: e.memset(sm[:, 7:8], 1e-5), writes=[R_sm])

    def rstd_from(bank, ncols, eps, msb, R_msb):
        ecol = sm[:, 6:7] if eps == 1e-6 else sm[:, 7:8]
        S.add(ACT, lambda e: e.activation(msb[:, 0:ncols], ps[bank][:, 0:ncols], AF.Ln, bias=ecol, scale=1.0),
              reads=[R_ps[bank], R_sm], writes=[R_msb])
        S.add(ACT, lambda e: e.activation(msb[:, 0:ncols], msb[:, 0:ncols], AF.Exp, scale=-0.5),
              reads=[R_msb], writes=[R_msb])

    A.off = OFF_T
    xt = [A.alloc(f"xt{i}", [128, 8, 512], F32) for i in range(2)]
    sqb = [A.alloc(f"sqb{i}", [128, 8, 512], BF16) for i in range(2)]
    msb = [A.alloc(f"msb{i}", [128, 512], F32) for i in range(2)]
    R_xt = [Res("xt0"), Res("xt1")]
    R_sqb = [Res("sqb0"), Res("sqb1")]
    R_msb = [Res("msb0"), Res("msb1")]
    def p1a(c):
        b = c % 2
        S.add(SP, lambda e, c=c, b=b: e.dma_start(out=xt[b][:, :, :], in_=xT3[:, :, c * 512:(c + 1) * 512]),
              writes=[R_xt[b]], dma_key=f"xt{b}")
        for k in range(8):
            S.add(ACT, lambda e, b=b, k=k: e.activation(sqb[b][:, k, :], xt[b][:, k, :], AF.Square),
                  reads=[R_xt[b]], writes=[R_sqb[b]])
        for k in range(8):
            S.add(PE, lambda e, b=b, k=k: e.matmul(ps[7][:, :], ones_d, sqb[b][:, k, :], start=(k == 0), stop=(k == 7)),
                  reads=[R_sqb[b], R_cb], writes=[R_ps[7]])
        rstd_from(7, 512, 1e-6, msb[b], R_msb[b])
        for k in range(8):
            S.add(DVE, lambda e, b=b, k=k, c=c: e.scalar_tensor_tensor(
                hT[:, k, 1 + c * 512:1 + (c + 1) * 512], xt[b][:, k, :], pcol(P_GPRE + k), msb[b][:, :],
                op0=ALU.mult, op1=ALU.mult),
                reads=[R_xt[b], R_msb[b], R_par], writes=[R_h[c]])

    def win(w):
        c0 = OWN * w
        n = min(512, S_LEN + 2 - c0)
        return c0, n

    def chunks_of(c0, n):
        t0 = max(c0 - 1, 0)
        t1 = min(c0 + n - 2, S_LEN - 1)
        return list(range(t0 // 512, t1 // 512 + 1))

    off_p1a_end = A.off
    A.off = OFF_YA
    wc = A.alloc("wc", [128, 8, 1536], BF16)
    R_wc = Res("wc")
    for k in range(8):
        S.add(POOL, lambda e, k=k: e.dma_start(out=wc[:, k, :], in_=w_in[k * 128:(k + 1) * 128, 0:1536]),
              writes=[R_wc], dma_key="wc")
    for k in range(8):
        S.add(POOL, lambda e, k=k: e.dma_start(out=win_s[:, k, 1536:3072], in_=w_in[k * 128:(k + 1) * 128, 1536:3072]),
              writes=[R_scr["win_s"]], dma_key="win_s")
    NB1 = 2
    hsb = [A.alloc(f"hsb{i}", [128, 512], F32) for i in range(NB1)]
    chb = [A.alloc(f"chb{i}", [128, 512], F32) for i in range(NB1)]
    assert A.off <= OFF_H
    A.off = off_p1a_end
    zb = [A.alloc(f"zb{i}", [128, 512], F32) for i in range(NB1)]
    yb = [A.alloc(f"yb{i}", [128, 512], F32) for i in range(NB1)]
    ysq = [A.alloc(f"ysq{i}", [128, 512], BF16) for i in range(NB1)]
    rb = [A.alloc(f"rb{i}", [128, 512], F32) for i in range(NB1)]
    bsb = [A.alloc(f"bsb{i}", [128, 512], F32) for i in range(NB1)]
    R_bsb = [Res(f"bsb{i}") for i in range(NB1)]
    R_hsb = [Res(f"hsb{i}") for i in range(NB1)]
    R_chb = [Res(f"chb{i}") for i in range(NB1)]
    R_zb = [Res(f"zb{i}") for i in range(NB1)]
    R_yb = [Res(f"yb{i}") for i in range(NB1)]
    R_ysq = [Res(f"ysq{i}") for i in range(NB1)]
    R_rb = [Res(f"rb{i}") for i in range(NB1)]

    its = [(w, fc) for w in range(NWIN) for fc in range(4)]

    def p1b_mm(i):
        w, fc = its[i]
        c0, n = win(w)
        hres = [R_h[c] for c in chunks_of(c0, n)] + [R_hpad]
        for g in range(3):
            bank = (i % 2) * 3 + g
            for k in range(8):
                S.add(PE, lambda e, bank=bank, g=g, k=k, fc=fc, c0=c0, n=n: e.matmul(
                    ps[bank][:, 0:n], wc[:, k, g * 512 + fc * 128:g * 512 + (fc + 1) * 128], hT[:, k, c0:c0 + n],
                    start=(k == 0), stop=(k == 7)),
                    reads=[R_wc] + hres, writes=[R_ps[bank]])

    def p1b_chain(i):
        w, fc = its[i]
        c0, n = win(w)
        b = i % NB1
        bB, bC, bH = (i % 2) * 3, (i % 2) * 3 + 1, (i % 2) * 3 + 2
        m = n - 2
        S.add(ACT, lambda e: e.activation(hsb[b][:, 0:n], ps[bH][:, 0:n], AF.Copy), reads=[R_ps[bH]], writes=[R_hsb[b]])
        S.add(ACT, lambda e: e.activation(bsb[b][:, 0:n], ps[bB][:, 0:n], AF.Copy), reads=[R_ps[bB]], writes=[R_bsb[b]])
        S.add(DVE, lambda e: e.tensor_tensor(chb[b][:, 0:n], ps[bC][:, 0:n], hsb[b][:, 0:n], op=ALU.mult),
              reads=[R_ps[bC], R_hsb[b]], writes=[R_chb[b]])
        S.add(DVE, lambda e: e.tensor_scalar(zb[b][:, 0:m], chb[b][:, 1:n - 1], pcol(P_CW + fc * 3 + 1), None, op0=ALU.mult),
              reads=[R_chb[b], R_par], writes=[R_zb[b]])
        S.add(DVE, lambda e: e.scalar_tensor_tensor(zb[b][:, 0:m], chb[b][:, 0:m], pcol(P_CW + fc * 3 + 0), zb[b][:, 0:m],
                                                     op0=ALU.mult, op1=ALU.add),
              reads=[R_chb[b], R_zb[b], R_par], writes=[R_zb[b]])
        S.add(DVE, lambda e: e.scalar_tensor_tensor(zb[b][:, 0:m], chb[b][:, 2:n], pcol(P_CW + fc * 3 + 2), zb[b][:, 0:m],
                                                    op0=ALU.mult, op1=ALU.add),
              reads=[R_chb[b], R_zb[b], R_par], writes=[R_zb[b]])
        S.add(DVE, lambda e: e.tensor_tensor(yb[b][:, 0:m], bsb[b][:, 1:n - 1], zb[b][:, 0:m], op=ALU.mult),
              reads=[R_bsb[b], R_zb[b]], writes=[R_yb[b]])
        S.add(ACT, lambda e: e.activation(ysq[b][:, 0:m], yb[b][:, 0:m], AF.Square), reads=[R_yb[b]], writes=[R_ysq[b]])

    def p1b_stat(i):
        w, fc = its[i]
        c0, n = win(w)
        b = i % NB1
        m = n - 2
        S.add(PE, lambda e: e.matmul(ps[6][:, 0:m], blk64, ysq[b][:, 0:m], start=True, stop=True),
              reads=[R_ysq[b], R_cb], writes=[R_ps[6]])

    def p1b_rstd(i):
        w, fc = its[i]
        c0, n = win(w)
        rstd_from(6, n - 2, 1e-6, rb[i % NB1], R_rb[i % NB1])

    def p1b_out(i):
        w, fc = its[i]
        c0, n = win(w)
        b = i % NB1
        m = n - 2
        yres = [R_yc[fc][c] for c in chunks_of(c0 + 1, m)]
        S.add(DVE, lambda e: e.scalar_tensor_tensor(yconv[:, fc, c0 + 1:c0 + 1 + m], yb[b][:, 0:m], pcol(P_GCONV + fc),
                                                    rb[b][:, 0:m], op0=ALU.mult, op1=ALU.mult),
              reads=[R_yb[b], R_rb[b], R_par], writes=yres)

    def p1b_iter(i):
        if i == 0:
            p1b_mm(0)
        if i + 1 < len(its):
            p1b_mm(i + 1)
        if i > 1:
            p1b_rstd(i - 2)
        if i > 0:
            p1b_stat(i - 1)
        if i > 1:
            p1b_out(i - 2)
        p1b_chain(i)

    p1a(0)
    for c in range(1, NCH):
        p1a(c)
        for i in range(4 * (c - 1), 4 * c):
            p1b_iter(i)
    for i in range(4 * (NCH - 1), len(its)):
        p1b_iter(i)
    nl = len(its)
    p1b_rstd(nl - 2)
    p1b_stat(nl - 1)
    p1b_out(nl - 2)
    p1b_rstd(nl - 1)
    p1b_out(nl - 1)
    if stop_after == "p1b":
        S.finish(); S.emit(); return nc
    S.barrier()
    S.add(POOL, lambda e: e.memset(yatt[:, :, 0:1], 0.0), writes=[R_ypad])
    S.add(POOL, lambda e: e.memset(yatt[:, :, S_LEN + 1:S_LEN + 2], 0.0), writes=[R_ypad])

    A.off = OFF_T
    kT = [A.alloc(f"kT{m}", [128, S_LEN], BF16) for m in range(2)]
    vt = A.alloc("vt", [128, S_LEN], BF16)
    wq = [A.alloc(f"wq{i}", [128, 8, 384], BF16) for i in range(2)]
    qA = [[A.alloc(f"qA{m}{b}", [128, 512], BF16) for b in range(2)] for m in range(2)]
    qB = [[A.alloc(f"qB{m}{b}", [128, 512], BF16) for b in range(2)] for m in range(2)]
    NPT = 8
    pT = [A.alloc(f"pT{i}", [128, 512], BF16) for i in range(NPT)]
    ocp = [A.alloc(f"ocp{m}", [128, 512], F32) for m in range(2)]
    zcp = [A.alloc(f"zcp{m}", [128, 512], F32) for m in range(2)]
    osq = A.alloc("osq", [128, 512], BF16)
    rr = A.alloc("rr", [128, 512], F32)
    zacc = [A.alloc(f"zacc{m}", [128, 512], F32) for m in range(2)]
    ones32 = A.alloc("ones32", [128, 128], F32)
    R_zacc = [Res("zacc0"), Res("zacc1")]
    R_o32 = Res("ones32")
    S.add(DVE, lambda e: e.memset(ones32[:, :], 1.0), writes=[R_o32])
    R_kd = [[Res(f"kd{m}_{c}") for c in range(NCH)] for m in range(2)]
    R_kp = [Res("kp0"), Res("kp1")]
    R_vt = [Res(f"vt{g}") for g in range(NCH)]
    R_wq = [Res("wq0"), Res("wq1")]
    R_qd = [[Res(f"qd{m}{b}") for b in range(2)] for m in range(2)]
    R_qp = [[Res(f"qp{m}{b}") for b in range(2)] for m in range(2)]
    R_pT = [Res(f"pT{i}") for i in range(NPT)]
    R_ocp = [Res("ocp0"), Res("ocp1")]
    R_zcp = [Res("zcp0"), Res("zcp1")]
    R_osq, R_rr = Res("osq"), Res("rr")
    OB, ZB = (4, 5), (6, 7)
    sctr = [0]
    sbank_of = {}

    def next_sbank():
        b_ = sctr[0] % 4
        sctr[0] += 1
        return b_

    def load_wq(h):
        b = h % 2
        for i, col in enumerate((1536, 2048, 2560)):
            S.add(SP, lambda e, b=b, i=i, col=col, h=h: e.dma_start(
                out=wq[b][:, :, i * 128:(i + 1) * 128], in_=win_s[:, :, col + h * 128:col + (h + 1) * 128]),
                reads=[R_scr["win_s"]], writes=[R_wq[b]], dma_key=f"wq{b}")

    def kv_gen(h):
        b = h % 2
        for m in range(2):
            S.add(SP, lambda e, m=m, h=h: e.dma_start(out=kT[m][64:72, :], in_=kpos_d[h]),
                  writes=[R_kp[m]], dma_key=f"kp{m}")
        for c in range(NCH):
            bank = next_sbank()
            for k in range(8):
                S.add(PE, lambda e, bank=bank, k=k, c=c: e.matmul(
                    ps[bank][:, :], wq[b][:, k, 128:256], hT[:, k, 1 + c * 512:1 + (c + 1) * 512],
                    start=(k == 0), stop=(k == 7)),
                    reads=[R_wq[b], R_h[c]], writes=[R_ps[bank]])
            S.add(ACT, lambda e, bank=bank, c=c: e.activation(kT[0][0:64, c * 512:(c + 1) * 512], ps[bank][0:64, :], AF.Copy),
                  reads=[R_ps[bank]], writes=[R_kd[0][c]])
            S.add(DVE, lambda e, bank=bank, c=c: e.tensor_copy(kT[1][0:64, c * 512:(c + 1) * 512], ps[bank][64:128, :]),
                  reads=[R_ps[bank]], writes=[R_kd[1][c]])
        for g in range(NCH):
            bank = next_sbank()
            for j in range(4):
                tb = 4 * g + j
                for k in range(8):
                    S.add(PE, lambda e, bank=bank, k=k, j=j, tb=tb: e.matmul(
                        ps[bank][:, j * 128:(j + 1) * 128], hT[:, k, 1 + tb * 128:1 + (tb + 1) * 128], wq[b][:, k, 256:384],
                        start=(k == 0), stop=(k == 7)),
                        reads=[R_wq[b], R_h[g]], writes=[R_ps[bank]])
            if g % 2 == 0:
                S.add(ACT, lambda e, bank=bank, g=g: e.activation(vt[:, g * 512:(g + 1) * 512], ps[bank][:, :], AF.Copy),
                      reads=[R_ps[bank]], writes=[R_vt[g]])
            else:
                S.add(DVE, lambda e, bank=bank, g=g: e.tensor_copy(vt[:, g * 512:(g + 1) * 512], ps[bank][:, :]),
                      reads=[R_ps[bank]], writes=[R_vt[g]])

    def q_pos(h, qc):
        b = qc % 2
        for m in range(2):
            S.add(SP, lambda e, m=m, b=b, h=h, qc=qc: e.dma_start(out=qA[m][b][64:72, :], in_=qpa_d[h, :, qc * 512:(qc + 1) * 512]),
                  writes=[R_qp[m][b]], dma_key=f"qp{m}{b}")
            S.add(SP, lambda e, m=m, b=b, h=h, qc=qc: e.dma_start(out=qB[m][b][64:72, :], in_=qpb_d[h, :, qc * 512:(qc + 1) * 512]),
                  writes=[R_qp[m][b]], dma_key=f"qp{m}{b}")

    def q_gen(h, qc):
        b = qc % 2
        wb = h % 2
        GB = next_sbank()
        for k in range(8):
            S.add(PE, lambda e, k=k: e.matmul(ps[GB][:, :], wq[wb][:, k, 0:128], hT[:, k, 1 + qc * 512:1 + (qc + 1) * 512],
                                              start=(k == 0), stop=(k == 7)),
                  reads=[R_wq[wb], R_h[qc]], writes=[R_ps[GB]])
        for m in range(2):
            S.add(DVE, lambda e, m=m: e.tensor_copy(qA[m][b][0:64, :], ps[GB][m * 64:(m + 1) * 64, :]),
                  reads=[R_ps[GB]], writes=[R_qd[m][b]])
            S.add(DVE, lambda e, m=m: e.tensor_copy(qB[m][b][0:64, :], ps[GB][m * 64:(m + 1) * 64, :]),
                  reads=[R_ps[GB]], writes=[R_qd[m][b]])

    def qk(h, qc, i):
        kb, m = i // 2, i % 2
        b = qc % 2
        bank = next_sbank()
        sbank_of[(h, qc, i)] = bank
        j = kb - 4 * qc
        lhs = kT[m][0:72, kb * 128:(kb + 1) * 128]
        rd = [R_kd[m][kb // 4], R_kp[m], R_qd[m][b], R_qp[m][b]]

        def mmv(lo, hi, qt, start=True, stop=True):
            S.add(PE, lambda e: e.matmul(ps[bank][:, lo:hi], lhs, qt[0:72, lo:hi], start=start, stop=stop),
                  reads=rd, writes=[R_ps[bank]])
        if j < 0:
            mmv(0, 512, qA[m][b])
        elif j > 3:
            mmv(0, 512, qB[m][b])
        else:
            if j > 0:
                mmv(0, 128 * j, qB[m][b])
            mmv(128 * j, 128 * (j + 1), qA[m][b], True, False)
            S.add(PE, lambda e: e.matmul(ps[bank][:, 128 * j:128 * (j + 1)], ident, cbt[:, CB_CD + h * 128:CB_CD + (h + 1) * 128],
                                         start=False, stop=True),
                  reads=[R_cb], writes=[R_ps[bank]])
            if j < 3:
                mmv(128 * (j + 1), 512, qA[m][b])

    def expo(h, qc, i):
        bank = sbank_of[(h, qc, i)]
        pb = i % NPT
        S.add(ACT, lambda e: e.activation(pT[pb][:, :], ps[bank][:, :], AF.Exp, scale=0.125),
              reads=[R_ps[bank]], writes=[R_pT[pb]])

    def pv(i):
        kb, m = i // 2, i % 2
        pb = i % NPT
        S.add(PE, lambda e: e.matmul(ps[OB[m]][:, :], vt[:, kb * 128:(kb + 1) * 128], pT[pb][:, :], start=(kb == 0), stop=(kb == 31)),
              reads=[R_vt[kb // 4], R_pT[pb]], writes=[R_ps[OB[m]]])
        if kb % 2 == 1 and kb != 31:
            if kb == 1:
                S.add(DVE, lambda e: e.tensor_copy(zacc[m][:, :], pT[pb][:, :]), reads=[R_pT[pb]], writes=[R_zacc[m]])
            else:
                S.add(DVE, lambda e: e.tensor_tensor(zacc[m][:, :], zacc[m][:, :], pT[pb][:, :], op=ALU.add),
                      reads=[R_zacc[m], R_pT[pb]], writes=[R_zacc[m]])
        else:
            S.add(PE, lambda e: e.matmul(ps[ZB[m]][:, :], ones1, pT[pb][:, :], start=(kb == 0), stop=False),
                  reads=[R_cb, R_pT[pb]], writes=[R_ps[ZB[m]]])
        if kb == 31:
            S.add(PE, lambda e: e.matmul(ps[ZB[m]][:, :], ones32[:, :], zacc[m][:, :], start=False, stop=True),
                  reads=[R_o32, R_zacc[m]], writes=[R_ps[ZB[m]]])

    def fin_copy(m):
        S.add(DVE, lambda e: e.tensor_copy(ocp[m][:, :], ps[OB[m]][:, :]), reads=[R_ps[OB[m]]], writes=[R_ocp[m]])
        S.add(DVE, lambda e: e.tensor_copy(zcp[m][:, :], ps[ZB[m]][:, :]), reads=[R_ps[ZB[m]]], writes=[R_zcp[m]])

    def fin_rest(h, qc):
        steps = []
        for m in range(2):
            steps.append(lambda m=m: S.add(DVE, lambda e: e.reciprocal(zcp[m][:, :], zcp[m][:, :]),
                                           reads=[R_zcp[m]], writes=[R_zcp[m]]))
            steps.append(lambda m=m: S.add(DVE, lambda e: e.tensor_tensor(ocp[m][:, :], ocp[m][:, :], zcp[m][:, :], op=ALU.mult),
                                           reads=[R_ocp[m], R_zcp[m]], writes=[R_ocp[m]]))
        steps.append(lambda: S.add(DVE, lambda e: e.scalar_tensor_tensor(ocp[0][:, :], ocp[1][:, :], sm[:, 0:1], ocp[0][:, :],
                                                                         op0=ALU.mult, op1=ALU.add),
                                   reads=[R_ocp[0], R_ocp[1], R_sm], writes=[R_ocp[0]]))
        steps.append(lambda: S.add(DVE, lambda e: e.tensor_tensor(osq[:, :], ocp[0][:, :], ocp[0][:, :], op=ALU.mult),
                                   reads=[R_ocp[0]], writes=[R_osq]))
        return steps

    def fin_rest2(h, qc):
        GB = next_sbank()
        S.add(PE, lambda e: e.matmul(ps[GB][:, :], ones_h, osq[:, :], start=True, stop=True), reads=[R_osq, R_cb], writes=[R_ps[GB]])
        rstd_from(GB, 512, 1e-5, rr, R_rr)
        S.add(DVE, lambda e: e.scalar_tensor_tensor(yatt[:, h, 1 + qc * 512:1 + (qc + 1) * 512], ocp[0][:, :], sm[:, 1:2], rr[:, :],
                                                    op0=ALU.mult, op1=ALU.mult),
              reads=[R_ocp[0], R_rr, R_sm], writes=[R_ya[h][qc]])

    for k in range(8):
        S.add(POOL, lambda e, k=k: e.dma_start(out=wout_s[:, k, :], in_=w_out[k * 128:(k + 1) * 128, :]),
              writes=[R_scr["wout_s"]], dma_key="wout_s")
    for k in range(8):
        S.add(POOL, lambda e, k=k: e.dma_start(out=wup_s[:, k, :], in_=w_up[k * 128:(k + 1) * 128, :]),
              writes=[R_scr["wup_s"]], dma_key="wup_s")
    for j in range(22):
        S.add(POOL, lambda e, j=j: e.dma_start(out=wdn_s[:, j, :], in_=w_dn[j * 128:(j + 1) * 128, :]),
              writes=[R_scr["wdn_s"]], dma_key="wdn_s")
    LA = 3
    load_wq(0)
    pending = None
    fsteps = []
    for h in range(n_heads):
        if h + 1 < n_heads:
            load_wq(h + 1)
        if pending is not None:
            for f in fin_rest(*pending):
                f()
        kv_gen(h)
        if pending is not None:
            fin_rest2(*pending)
            pending = None
        q_pos(h, 0)
        q_gen(h, 0)
        for i in range(LA):
            qk(h, 0, i)
        for qc in range(NCH):
            if qc + 1 < NCH:
                q_pos(h, qc + 1)
            for i in range(64):
                expo(h, qc, i)
                pv(i)
                if i >= 62:
                    fin_copy(i - 62)
                if i + LA < 64:
                    qk(h, qc, i + LA)
                elif qc + 1 < NCH:
                    qk(h, qc + 1, i + LA - 64)
                if i == 2 and pending is not None:
                    fsteps = fin_rest(*pending)
                if pending is not None and i >= 2 and (i - 2) % 4 == 0 and fsteps:
                    fsteps.pop(0)()
                if i == 30 and pending is not None:
                    fin_rest2(*pending)
                    pending = None
                if i == 36 and qc + 1 < NCH:
                    q_gen(h, qc + 1)
            pending = (h, qc)
    for f in fin_rest(*pending):
        f()
    fin_rest2(*pending)
    if stop_after == "p2":
        S.finish(); S.emit(); return nc
    S.barrier()

    A.off = OFF_H
    xw = [A.alloc(f"xw{i}", [128, 8, 512], F32) for i in range(2)]
    mb = [A.alloc(f"mb{i}", [128, 8, 512], F32) for i in range(2)]
    h2 = A.alloc("h2", [128, 8, 512], BF16)
    at = A.alloc("at", [128, 22, 512], BF16)
    NWT = 5
    wt = [A.alloc(f"wt{i}", [128, 8, 128], BF16) for i in range(NWT)]
    wd = [A.alloc(f"wd{i}", [128, 22, 128], BF16) for i in range(2)]
    g0 = [A.alloc(f"g0{i}", [128, 512], F32) for i in range(2)]
    u0 = [A.alloc(f"u0{i}", [128, 512], F32) for i in range(2)]
    NSQ = 2
    sqt = [A.alloc(f"sqt{i}", [128, 512], BF16) for i in range(NSQ)]
    rs3 = [A.alloc(f"rs3{i}", [128, 512], F32) for i in range(2)]
    rs3 = [rs3[0], rs3[0], rs3[1]]
    sqd = [A.alloc(f"sqd{i}", [128, 512], BF16) for i in range(4)]
    R_sqd = [Res(f"sqd{i}") for i in range(4)]
    R_xw = [[Res(f"xw{i}_{m}") for m in range(8)] for i in range(2)]
    R_mb = [[Res(f"mb{i}_{m}") for m in range(8)] for i in range(2)]
    R_h2 = [Res(f"h2{m}") for m in range(8)]
    R_at = [Res(f"at{j}") for j in range(22)]
    R_wt = [Res(f"wt{i}") for i in range(NWT)]
    R_wd = [Res("wd0"), Res("wd1")]
    R_g0 = [Res("g00"), Res("g01")]
    R_u0 = [Res("u00"), Res("u01")]
    R_sqt = [Res(f"sqt{i}") for i in range(NSQ)]
    sqx = [A.alloc(f"sqx{i}", [128, 512], BF16) for i in range(3)]
    R_sqx = [Res(f"sqx{i}") for i in range(3)]
    sqxc = [0]
    R_rs3 = [Res(f"rs3{i}") for i in range(2)]
    R_rs3 = [R_rs3[0], R_rs3[0], R_rs3[1]]
    sqc = [0]

    def next_sq():
        i_ = sqc[0] % NSQ
        sqc[0] += 1
        return i_

    wseq = []
    for w in range(NWIN):
        for m in range(8):
            wseq.append(("wout_s", wout_s[:, :, m * 128:(m + 1) * 128]))
        for j in range(22):
            wseq.append(("wup_s", wup_s[:, :, j * 128:(j + 1) * 128]))
            wseq.append(("wup_s", wup_s[:, :, DFF + j * 128:DFF + (j + 1) * 128]))
    wissued = [0]
    WLA = 4

    def wtile(idx):
        while wissued[0] < len(wseq) and wissued[0] <= idx + WLA:
            k_ = wissued[0]
            b_ = k_ % NWT
            key, src = wseq[k_]
            S.add(SP, lambda e, b_=b_, src=src: e.dma_start(out=wt[b_][:, :, :], in_=src),
                  reads=[R_scr[key]], writes=[R_wt[b_]], dma_key=f"wt{b_}")
            wissued[0] += 1
        return wt[idx % NWT], R_wt[idx % NWT]

    def widx_m(w, m):
        return w * 52 + m

    def widx_u(w, j, which):
        return w * 52 + 8 + 2 * j + which

    dissued = [0]

    def dtile(idx):
        while dissued[0] < NWIN * 8 and dissued[0] <= idx + 1:
            k_ = dissued[0]
            b_ = k_ % 2
            m = k_ % 8
            S.add(SP, lambda e, b_=b_, m=m: e.dma_start(out=wd[b_][:, :, :], in_=wdn_s[:, :, m * 128:(m + 1) * 128]),
                  reads=[R_scr["wdn_s"]], writes=[R_wd[b_]], dma_key=f"wd{b_}")
            dissued[0] += 1
        return wd[idx % 2], R_wd[idx % 2]

    def geom(w):
        c0, n = win(w)
        return c0, n, n - 2

    def p3_LX(w):
        c0, n, m_ = geom(w)
        xb = xw[w % 2]
        tok_lo = max(c0 - 1, 0)
        tok_hi = min(c0 - 1 + n, S_LEN)
        col_lo = tok_lo - (c0 - 1)
        ncol = tok_hi - tok_lo
        if col_lo > 0:
            S.add(POOL, lambda e: e.memset(xb[:, :, 0:col_lo], 0.0), writes=R_xw[w % 2])
        if col_lo + ncol < n:
            S.add(POOL, lambda e: e.memset(xb[:, :, col_lo + ncol:n], 0.0), writes=R_xw[w % 2])
        S.add(SP, lambda e: e.dma_start(out=xb[:, :, col_lo:col_lo + ncol], in_=xT3[:, :, tok_lo:tok_hi]),
              writes=R_xw[w % 2], dma_key=f"xw{w % 2}")

    def stat_mm(bank, si, ncols, first, last, pool=None):
        tl, rl = (sqt, R_sqt) if pool is None else pool
        S.add(PE, lambda e: e.matmul(ps[bank][:, 0:ncols], ones_d, tl[si][:, 0:ncols], start=first, stop=last),
              reads=[rl[si], R_cb], writes=[R_ps[bank]])

    qc_ = []
    q3_ = []

    def flush(q, keep=0):
        while len(q) > keep:
            q.pop(0)()

    def p3_M(w):
        c0, n, m_ = geom(w)
        mbw, R_mbw = mb[w % 2], R_mb[w % 2]
        ycr = [[R_yc[k][c] for c in chunks_of(c0, n)] + [R_ypad] for k in range(4)]
        yar = [[R_ya[k][c] for c in chunks_of(c0, n)] + [R_ypad] for k in range(4)]
        prev = None
        for m in range(8):
            bank = m % 4
            wtl, rw = wtile(widx_m(w, m))
            for k in range(8):
                src = yconv if k < 4 else yatt
                rres = ycr[k] if k < 4 else yar[k - 4]
                S.add(PE, lambda e, bank=bank, k=k, src=src, wtl=wtl: e.matmul(
                    ps[bank][:, 0:n], wtl[:, k, :], src[:, k % 4, c0:c0 + n], start=(k == 0), stop=(k == 7)),
                    reads=[rw] + rres, writes=[R_ps[bank]])
            flush(qc_, 1)
            S.add(DVE, lambda e, bank=bank, m=m: e.tensor_copy(mbw[:, m, 0:n], ps[bank][:, 0:n]),
                  reads=[R_ps[bank]], writes=[R_mbw[m]])
            si = next_sq()
            S.add(ACT, lambda e, m=m, si=si: e.activation(sqt[si][:, 0:n], mbw[:, m, 0:n], AF.Square),
                  reads=[R_mbw[m]], writes=[R_sqt[si]])
            qc_.append(lambda si=si, m=m: stat_mm(6, si, n, m == 0, m == 7))

    def chain_slices(w):
        c0, n, m_ = geom(w)
        xb, R_xb = xw[w % 2], R_xw[w % 2]
        mbw, R_mbw = mb[w % 2], R_mb[w % 2]
        sl = []
        sis = {}

        def x1_ops(ms, with_rstd):
            def f():
                flush(qc_)
                if with_rstd:
                    rstd_from(6, n, 1e-6, rs3[0], R_rs3[0])
                for m in ms:
                    S.add(POOL, lambda e, m=m: e.tensor_tensor(mbw[:, m, 0:n], mbw[:, m, 0:n], rs3[0][:, 0:n], op=ALU.mult),
                          reads=[R_mbw[m], R_rs3[0]], writes=[R_mbw[m]])
                    S.add(DVE, lambda e, m=m: e.scalar_tensor_tensor(xb[:, m, 0:n], mbw[:, m, 0:n], pcol(P_GPOST + m), xb[:, m, 0:n],
                                                                     op0=ALU.mult, op1=ALU.add),
                          reads=[R_mbw[m], R_xb[m], R_par], writes=[R_xb[m]])
                    si = sqxc[0] % 3
                    sqxc[0] += 1
                    sis[m] = si
                    S.add(ACT, lambda e, m=m, si=si: e.activation(sqx[si][:, 0:n], xb[:, m, 0:n], AF.Square),
                          reads=[R_xb[m]], writes=[R_sqx[si]])
            return f

        def x1_pe(ms):
            def f():
                for m in ms:
                    qc_.append(lambda m=m: stat_mm(7, sis[m], n, m == 0, m == 7, pool=(sqx, R_sqx)))
            return f

        def h2_ops(ms, with_rstd):
            def f():
                if with_rstd:
                    flush(qc_)
                    rstd_from(7, n, 1e-6, rs3[1], R_rs3[1])
                for m in ms:
                    S.add(DVE, lambda e, m=m: e.scalar_tensor_tensor(h2[:, m, 0:n], xb[:, m, 0:n], pcol(P_GFPRE + m), rs3[1][:, 0:n],
                                                                     op0=ALU.mult, op1=ALU.mult),
                          reads=[R_xb[m], R_rs3[1], R_par], writes=[R_h2[m]])
            return f
        sl.append((x1_ops([0, 1], True), x1_pe([0, 1])))
        sl.append((x1_ops([2, 3, 4], False), x1_pe([2, 3, 4])))
        sl.append((x1_ops([5, 6, 7], False), x1_pe([5, 6, 7])))
        sl.append((h2_ops([0, 1, 2, 3], True), None))
        sl.append((h2_ops([4, 5, 6, 7], False), None))
        return sl

    def p3_U(w):
        c0, n, m_ = geom(w)

        def up_mm(j):
            for which, bank in ((0, (j % 2) * 2), (1, (j % 2) * 2 + 1)):
                wtl, rw = wtile(widx_u(w, j, which))
                for k in range(8):
                    S.add(PE, lambda e, bank=bank, wtl=wtl, k=k: e.matmul(ps[bank][:, 0:n], wtl[:, k, :], h2[:, k, 0:n],
                                                                          start=(k == 0), stop=(k == 7)),
                          reads=[rw, R_h2[k]], writes=[R_ps[bank]])

        def up_ew(j):
            b2 = j % 2
            bg, bu = (j % 2) * 2, (j % 2) * 2 + 1
            for (bank, dst, rd, cidx) in ((bg, g0[b2], R_g0[b2], j), (bu, u0[b2], R_u0[b2], 22 + j)):
                S.add(ACT, lambda e, bank=bank, dst=dst, cidx=cidx: e.activation(
                    dst[:, 0:m_], ps[bank][:, 1:n - 1], AF.Identity, bias=pcol(P_FB + cidx), scale=pcol(P_FW + cidx * 3 + 1)),
                    reads=[R_ps[bank], R_par], writes=[rd])
                S.add(DVE, lambda e, bank=bank, dst=dst, cidx=cidx: e.scalar_tensor_tensor(
                    dst[:, 0:m_], ps[bank][:, 0:m_], pcol(P_FW + cidx * 3 + 0), dst[:, 0:m_], op0=ALU.mult, op1=ALU.add),
                    reads=[R_ps[bank], rd, R_par], writes=[rd])
                S.add(DVE, lambda e, bank=bank, dst=dst, cidx=cidx: e.scalar_tensor_tensor(
                    dst[:, 0:m_], ps[bank][:, 2:n], pcol(P_FW + cidx * 3 + 2), dst[:, 0:m_], op0=ALU.mult, op1=ALU.add),
                    reads=[R_ps[bank], rd, R_par], writes=[rd])
            S.add(ACT, lambda e: e.activation(g0[b2][:, 0:m_], g0[b2][:, 0:m_], AF.Silu), reads=[R_g0[b2]], writes=[R_g0[b2]])
            S.add(POOL, lambda e: e.tensor_tensor(at[:, j, 0:m_], g0[b2][:, 0:m_], u0[b2][:, 0:m_], op=ALU.mult),
                  reads=[R_g0[b2], R_u0[b2]], writes=[R_at[j]])

        def head():
            up_mm(0)
            up_mm(1)

        def rest(hooks=()):
            hooks = list(hooks)
            for j in range(22):
                hk = hooks.pop(0) if hooks else (None, None)
                if hk[0] is not None:
                    hk[0]()
                up_ew(j)
                if j + 2 < 22:
                    up_mm(j + 2)
                if hk[1] is not None:
                    hk[1]()
            assert not hooks
        return head, rest

    def p3_D(w, slices):
        c0, n, m_ = geom(w)
        mbw, R_mbw = mb[w % 2], R_mb[w % 2]
        have_next = w + 1 < NWIN
        rstd1_done = not have_next
        for m in range(8):
            if m == 2 and have_next:
                p3_M(w + 1)
            bank = 4 + (m % 2)
            wdl, rwd = dtile(w * 8 + m)
            for j in range(22):
                S.add(PE, lambda e, bank=bank, wdl=wdl, j=j: e.matmul(ps[bank][:, 0:m_], wdl[:, j, :], at[:, j, 0:m_],
                                                                      start=(j == 0), stop=(j == 21)),
                      reads=[rwd, R_at[j]], writes=[R_ps[bank]])
            flush(qc_, 1)
            if rstd1_done:
                flush(q3_, 1)
            S.add(DVE, lambda e, bank=bank, m=m: e.tensor_copy(mbw[:, m, 1:1 + m_], ps[bank][:, 0:m_]),
                  reads=[R_ps[bank]], writes=[R_mbw[m]])
            si = m % 4
            S.add(ACT, lambda e, m=m, si=si: e.activation(sqd[si][:, 0:m_], mbw[:, m, 1:1 + m_], AF.Square),
                  reads=[R_mbw[m]], writes=[R_sqd[si]])
            if m >= 2 and slices:
                nf, pf = slices.pop(0)
                nf()
                if pf is not None:
                    pf()
                rstd1_done = True
            q3_.append(lambda si=si, m=m: stat_mm(6, si, m_, m == 0, m == 7, pool=(sqd, R_sqd)))
        assert not slices

    def p3_E(w):
        c0, n, m_ = geom(w)
        xb, R_xb = xw[w % 2], R_xw[w % 2]
        mbw, R_mbw = mb[w % 2], R_mb[w % 2]

        def head():
            flush(qc_)
            flush(q3_)
            rstd_from(6, m_, 1e-6, rs3[2], R_rs3[2])

        def part(m):
            def pre():
                S.add(POOL, lambda e: e.tensor_tensor(mbw[:, m, 1:1 + m_], mbw[:, m, 1:1 + m_], rs3[2][:, 0:m_], op=ALU.mult),
                      reads=[R_mbw[m], R_rs3[2]], writes=[R_mbw[m]])

            def post():
                S.add(DVE, lambda e: e.scalar_tensor_tensor(mbw[:, m, 1:1 + m_], mbw[:, m, 1:1 + m_], pcol(P_GFPOST + m),
                                                            xb[:, m, 1:1 + m_], op0=ALU.mult, op1=ALU.add),
                      reads=[R_mbw[m], R_xb[m], R_par], writes=[R_mbw[m]])
            return (pre, post)

        def tail():
            t0 = OWN * w
            S.add(SP, lambda e: e.dma_start(out=yT3[:, :, t0:t0 + m_], in_=mbw[:, :, 1:1 + m_]),
                  reads=R_mbw, dma_key=f"out{w % 2}")
        return [(head, None)] + [part(m) for m in range(8)] + [(None, tail)]

    p3_LX(0)
    p3_M(0)
    for nf, pf in chain_slices(0):
        nf()
        if pf is not None:
            pf()
    flush(qc_)
    p3_LX(1)
    uh, ur = p3_U(0)
    uh()
    ur()
    for w in range(NWIN):
        sl = chain_slices(w + 1) if w + 1 < NWIN else []
        p3_D(w, sl)
        steps = p3_E(w)
        if w + 1 < NWIN:
            uh, ur = p3_U(w + 1)
            uh()
            if w + 2 < NWIN:
                steps.append((None, lambda w=w: p3_LX(w + 2)))
            ur(steps)
        else:
            for pre_, post_ in steps:
                if pre_ is not None:
                    pre_()
                if post_ is not None:
                    post_()

    S.finish()
    S.emit()
    return nc


_CACHE = {}


def kernel(x, g_mix_pre, w_mix_in, conv_w, g_conv_out, lambda_q1, lambda_k1, lambda_q2, lambda_k2,
           g_subln, w_mix_out, g_mix_post, g_ffn_pre, w_ffn_up, ffn_conv_w, ffn_conv_b, w_ffn_down,
           g_ffn_post):
    f32 = np.float32
    x = np.asarray(x, f32)
    par = pack_params(*[np.asarray(a, f32) for a in (
        g_mix_pre, conv_w, g_conv_out, lambda_q1, lambda_k1, lambda_q2, lambda_k2, g_subln,
        g_mix_post, g_ffn_pre, ffn_conv_w, ffn_conv_b, g_ffn_post)])
    cb, kpos, qpa, qpb = const_tables()
    w_in = np.ascontiguousarray(np.asarray(w_mix_in, f32)[0])
    w_out = np.ascontiguousarray(np.asarray(w_mix_out, f32)[0])
    w_up = np.ascontiguousarray(np.asarray(w_ffn_up, f32)[0])
    w_dn = np.ascontiguousarray(np.asarray(w_ffn_down, f32)[0])
    if "nc" not in _CACHE:
        _CACHE["nc"] = build_program()
    nc = _CACHE["nc"]
    in_maps = []
    for b in range(8):
        in_maps.append({
            "xT": np.ascontiguousarray(x[b].T),
            "w_in": w_in, "w_out": w_out, "w_up": w_up, "w_dn": w_dn,
            "par": par, "cb": cb, "kpos": kpos, "qpa": qpa, "qpb": qpb,
        })
    res = run_bass_kernel_spmd(nc, in_maps, core_ids=list(range(8)))
    out = np.stack([np.asarray(r["yT"], f32).T for r in res.results], axis=0)
    return np.ascontiguousarray(out)
```
